# Optimizing a Trainium2 kernel written in Bass

```python
import math
import jax, jax.numpy as jnp
from jax import lax
import numpy as np

D_MODEL = 1024
BATCH = 8
SEQ = 2048
DEPTH = 1
DEC_BATCH = 128
DEC_SEQ = 4
PAST_LEN = 16384
PAGE_SIZE = 128

MIX_WIDTH = D_MODEL
DN_WIDTH = MIX_WIDTH // 2
DN_HEADS = 4
DN_HEAD_DIM = DN_WIDTH // DN_HEADS
GLA_WIDTH = MIX_WIDTH - DN_WIDTH
GLA_HEADS = 4
GLA_HEAD_DIM = GLA_WIDTH // GLA_HEADS
GLA_GATE_RANK = 16
GLA_GATE_NORM = 16.0
CONV_WIDTH = 4
DN_CHUNK = 64
GLA_CHUNK = 16
D_FF = 4 * D_MODEL
EPS = 1e-6
IN_SIZES = (3 * DN_WIDTH, DN_WIDTH, DN_HEADS, DN_HEADS, GLA_WIDTH, GLA_WIDTH, GLA_WIDTH, GLA_WIDTH, GLA_GATE_RANK)
IN_COLS = 3 * DN_WIDTH + DN_WIDTH + 2 * DN_HEADS + 4 * GLA_WIDTH + GLA_GATE_RANK

kernel_name = "hymba_gdn_gla_adaln_step"


def rms_norm(x, w):
    xf = x.astype(jnp.float32)
    y = xf * lax.rsqrt(jnp.mean(xf * xf, axis=-1, keepdims=True) + EPS)
    return (y * w.astype(jnp.float32)).astype(x.dtype)


def l2norm(x):
    return x * lax.rsqrt(jnp.sum(x * x, axis=-1, keepdims=True) + EPS)


def gated_head_norm(o, gate, w):
    o = o * lax.rsqrt(jnp.mean(o * o, axis=-1, keepdims=True) + EPS)
    return o * w.astype(jnp.float32) * jax.nn.silu(gate.astype(jnp.float32))


def short_conv(u, buf, w):
    L = u.shape[1]
    up = jnp.concatenate([buf.astype(u.dtype), u], axis=1)
    out = up[:, 0:L] * w[0]
    for i in range(1, CONV_WIDTH):
        out = out + up[:, i:i + L] * w[i]
    return jax.nn.silu(out), up[:, up.shape[1] - (CONV_WIDTH - 1):]


def _pad_time(t, Lp):
    if Lp == t.shape[1]:
        return t
    widths = [(0, 0)] * t.ndim
    widths[1] = (0, Lp - t.shape[1])
    return jnp.pad(t, widths)


def _to_chunks(t, C):
    B, Lp, H = t.shape[:3]
    t = t.reshape((B, Lp // C, C, H) + t.shape[3:])
    return jnp.swapaxes(t, 2, 3)


def _from_chunks(t, L):
    B, N, H, C, d = t.shape
    return jnp.swapaxes(t, 2, 3).reshape(B, N * C, H, d)[:, :L]


def gated_delta_rule(q, k, v, g, beta, s0):
    B, L, H, dk = k.shape
    dv = v.shape[-1]
    C = min(DN_CHUNK, L)
    Lp = -(-L // C) * C
    q, k, v = (_to_chunks(_pad_time(t, Lp), C) for t in (q, k, v))
    g, beta = (_to_chunks(_pad_time(t, Lp)[..., None], C)[..., 0] for t in (g, beta))
    G = jnp.cumsum(g, axis=-1)
    idx = jnp.arange(C)
    incl = idx[:, None] >= idx[None, :]
    strict = idx[:, None] > idx[None, :]
    decay = jnp.exp(jnp.where(incl, G[..., :, None] - G[..., None, :], -jnp.inf))
    kb = k * beta[..., None]
    lower = jnp.where(strict, jnp.einsum('bnhid,bnhjd->bnhij', kb, k) * decay, 0.0)
    tmat = lower + jnp.eye(C, dtype=lower.dtype)
    rhs = jnp.concatenate([v * beta[..., None], kb * jnp.exp(G)[..., None]], axis=-1)
    sol = lax.linalg.triangular_solve(tmat, rhs, left_side=True, lower=True, unit_diagonal=True)
    w_val, w_key = sol[..., :dv], sol[..., dv:]
    attn = jnp.einsum('bnhid,bnhjd->bnhij', q, k) * decay
    q_dec = q * jnp.exp(G)[..., None]
    k_dec = k * jnp.exp(G[..., -1:] - G)[..., None]
    g_last = jnp.exp(G[..., -1])

    def step(S, xs):
        wv, wk, a, qd, kd, gl = xs
        u = wv - jnp.einsum('bhck,bhkv->bhcv', wk, S)
        o = jnp.einsum('bhck,bhkv->bhcv', qd, S) + jnp.einsum('bhij,bhjv->bhiv', a, u)
        S = S * gl[..., None, None] + jnp.einsum('bhck,bhcv->bhkv', kd, u)
        return S, o

    xs = tuple(jnp.moveaxis(t, 1, 0) for t in (w_val, w_key, attn, q_dec, k_dec, g_last))
    S, o = lax.scan(step, s0, xs)
    return _from_chunks(jnp.moveaxis(o, 0, 1), L), S


def gla_recurrence(q, k, v, glog, s0):
    B, L, H, dk = k.shape
    C = min(GLA_CHUNK, L)
    Lp = -(-L // C) * C
    q, k, v, glog = (_to_chunks(_pad_time(t, Lp), C) for t in (q, k, v, glog))
    Bc = jnp.cumsum(glog, axis=3)
    q_dec = q * jnp.exp(Bc)
    k_inv = k * jnp.exp(-Bc)
    idx = jnp.arange(C)
    incl = idx[:, None] >= idx[None, :]
    attn = jnp.where(incl, jnp.einsum('bnhik,bnhjk->bnhij', q_dec, k_inv), 0.0)
    o_intra = jnp.einsum('bnhij,bnhjv->bnhiv', attn, v)
    k_dec = k * jnp.exp(Bc[..., -1:, :] - Bc)
    g_last = jnp.exp(Bc[..., -1, :])

    def step(S, xs):
        qd, kd, vv, gl = xs
        o = jnp.einsum('bhck,bhkv->bhcv', qd, S)
        S = S * gl[..., :, None] + jnp.einsum('bhck,bhcv->bhkv', kd, vv)
        return S, o

    xs = tuple(jnp.moveaxis(t, 1, 0) for t in (q_dec, k_dec, v, g_last))
    S, o_inter = lax.scan(step, s0, xs)
    o = o_intra + jnp.moveaxis(o_inter, 0, 1)
    return _from_chunks(o, L), S


def hybrid_mixer(h, conv_buf, s_dn, s_gla, w_in, w_conv, dn_a_log, dn_dt_bias, w_dn_norm,
                 w_gla_g2, b_gla_g, w_gla_norm, w_o):
    B, L, _ = h.shape
    f32 = jnp.float32
    proj = h @ w_in
    cuts = []
    acc = 0
    for s in IN_SIZES[:-1]:
        acc += s
        cuts.append(acc)
    dn_qkv, dn_z, dn_a, dn_b, gq, gk, gv, gr, gg = jnp.split(proj, cuts, axis=-1)

    qkv, conv_new = short_conv(dn_qkv, conv_buf, w_conv)
    dq, dk_, dv_ = jnp.split(qkv.astype(f32), 3, axis=-1)
    dq = l2norm(dq.reshape(B, L, DN_HEADS, DN_HEAD_DIM)) * (DN_HEAD_DIM ** -0.5)
    dk_ = l2norm(dk_.reshape(B, L, DN_HEADS, DN_HEAD_DIM))
    dv_ = dv_.reshape(B, L, DN_HEADS, DN_HEAD_DIM)
    beta = jax.nn.sigmoid(dn_b.astype(f32))
    g = -jnp.exp(dn_a_log.astype(f32)) * jax.nn.softplus(dn_a.astype(f32) + dn_dt_bias.astype(f32))
    o_dn, s_dn_new = gated_delta_rule(dq, dk_, dv_, g, beta, s_dn.astype(f32))
    o_dn = gated_head_norm(o_dn, dn_z.reshape(B, L, DN_HEADS, DN_HEAD_DIM), w_dn_norm).reshape(B, L, DN_WIDTH)

    glog = jax.nn.log_sigmoid((gg @ w_gla_g2 + b_gla_g).astype(f32)) / GLA_GATE_NORM
    glog = glog.reshape(B, L, GLA_HEADS, GLA_HEAD_DIM)
    q_g = gq.astype(f32).reshape(B, L, GLA_HEADS, GLA_HEAD_DIM) * (GLA_HEAD_DIM ** -0.5)
    k_g = gk.astype(f32).reshape(B, L, GLA_HEADS, GLA_HEAD_DIM)
    v_g = gv.astype(f32).reshape(B, L, GLA_HEADS, GLA_HEAD_DIM)
    o_gla, s_gla_new = gla_recurrence(q_g, k_g, v_g, glog, s_gla.astype(f32))
    o_gla = gated_head_norm(o_gla, gr.reshape(B, L, GLA_HEADS, GLA_HEAD_DIM), w_gla_norm).reshape(B, L, GLA_WIDTH)

    out = jnp.concatenate([o_dn, o_gla], axis=-1).astype(h.dtype) @ w_o
    return out, conv_new, s_dn_new, s_gla_new


def trunk(x, c, conv_bufs, s_dns, s_glas, w_ada, b_ada, w_norm1, w_in, w_conv, dn_a_log, dn_dt_bias,
          w_dn_norm, w_gla_g2, b_gla_g, w_gla_norm, w_o, w_norm2, w_up, w_down, w_norm_f):
    new_conv, new_dn, new_gla = [], [], []
    cs = jax.nn.silu(c.astype(x.dtype))
    for layer in range(DEPTH):
        mod = (cs @ w_ada[layer] + b_ada[layer])[:, None, :]
        sh1, sc1, gt1, sh2, sc2, gt2 = jnp.split(mod, 6, axis=-1)
        h = rms_norm(x, w_norm1[layer]) * (1 + sc1) + sh1
        mix, cb, sd, sg = hybrid_mixer(h, conv_bufs[layer], s_dns[layer], s_glas[layer], w_in[layer],
                                       w_conv[layer], dn_a_log[layer], dn_dt_bias[layer], w_dn_norm[layer],
                                       w_gla_g2[layer], b_gla_g[layer], w_gla_norm[layer], w_o[layer])
        x = x + gt1 * mix
        h = rms_norm(x, w_norm2[layer]) * (1 + sc2) + sh2
        x = x + gt2 * (jnp.square(jax.nn.relu(h @ w_up[layer])) @ w_down[layer])
        new_conv.append(cb)
        new_dn.append(sd)
        new_gla.append(sg)
    y = rms_norm(x, w_norm_f)
    return (y, jnp.stack(new_conv).astype(x.dtype), jnp.stack(new_dn).astype(x.dtype),
            jnp.stack(new_gla).astype(x.dtype))


def setup_inputs(seed: int = 0) -> dict:
    key = jax.random.key(seed)
    ks = jax.random.split(key, 24)
    nrm = jax.random.normal
    dt = jnp.exp(jax.random.uniform(ks[10], (DEPTH, DN_HEADS)) * (math.log(0.1) - math.log(0.001)) + math.log(0.001))
    return {
        "x_prompt": nrm(ks[0], (BATCH, SEQ, D_MODEL), jnp.float32),
        "x_sample": nrm(ks[1], (DEC_BATCH, DEC_SEQ, D_MODEL), jnp.float32),
        "state_dn_conv": nrm(ks[2], (DEPTH, DEC_BATCH, CONV_WIDTH - 1, 3 * DN_WIDTH), jnp.float32),
        "state_dn": 0.5 * nrm(ks[3], (DEPTH, DEC_BATCH, DN_HEADS, DN_HEAD_DIM, DN_HEAD_DIM), jnp.float32),
        "state_gla": 0.5 * nrm(ks[4], (DEPTH, DEC_BATCH, GLA_HEADS, GLA_HEAD_DIM, GLA_HEAD_DIM), jnp.float32),
        "c_prompt": nrm(ks[5], (BATCH, D_MODEL), jnp.float32),
        "c_sample": nrm(ks[6], (DEC_BATCH, D_MODEL), jnp.float32),
        "w_ada": 0.5 * D_MODEL ** -0.5 * nrm(ks[7], (DEPTH, D_MODEL, 6 * D_MODEL), jnp.float32),
        "b_ada": 0.01 * nrm(ks[8], (DEPTH, 6 * D_MODEL), jnp.float32),
        "w_norm1": 1.0 + 0.02 * nrm(ks[9], (DEPTH, D_MODEL), jnp.float32),
        "w_in": D_MODEL ** -0.5 * nrm(ks[11], (DEPTH, D_MODEL, IN_COLS), jnp.float32),
        "w_conv": CONV_WIDTH ** -0.5 * nrm(ks[12], (DEPTH, CONV_WIDTH, 3 * DN_WIDTH), jnp.float32),
        "dn_a_log": jnp.log(jax.random.uniform(ks[13], (DEPTH, DN_HEADS), jnp.float32, 1.0, 16.0)),
        "dn_dt_bias": dt + jnp.log(-jnp.expm1(-dt)),
        "w_dn_norm": 1.0 + 0.02 * nrm(ks[14], (DEPTH, DN_HEAD_DIM), jnp.float32),
        "w_gla_g2": GLA_GATE_RANK ** -0.5 * nrm(ks[15], (DEPTH, GLA_GATE_RANK, GLA_WIDTH), jnp.float32),
        "b_gla_g": 0.1 * nrm(ks[16], (DEPTH, GLA_WIDTH), jnp.float32),
        "w_gla_norm": 1.0 + 0.02 * nrm(ks[17], (DEPTH, GLA_HEAD_DIM), jnp.float32),
        "w_o": MIX_WIDTH ** -0.5 * nrm(ks[18], (DEPTH, MIX_WIDTH, D_MODEL), jnp.float32),
        "w_norm2": 1.0 + 0.02 * nrm(ks[19], (DEPTH, D_MODEL), jnp.float32),
        "w_up": D_MODEL ** -0.5 * nrm(ks[20], (DEPTH, D_MODEL, D_FF), jnp.float32),
        "w_down": D_FF ** -0.5 * nrm(ks[21], (DEPTH, D_FF, D_MODEL), jnp.float32),
        "w_norm_f": 1.0 + 0.02 * nrm(ks[22], (D_MODEL,), jnp.float32),
    }


def reference(x_prompt, x_sample, state_dn_conv, state_dn, state_gla, c_prompt, c_sample, w_ada, b_ada,
              w_norm1, w_in, w_conv, dn_a_log, dn_dt_bias, w_dn_norm, w_gla_g2, b_gla_g, w_gla_norm, w_o,
              w_norm2, w_up, w_down, w_norm_f):
    params = (w_ada, b_ada, w_norm1, w_in, w_conv, dn_a_log, dn_dt_bias, w_dn_norm, w_gla_g2, b_gla_g,
              w_gla_norm, w_o, w_norm2, w_up, w_down, w_norm_f)
    nb = x_prompt.shape[0]
    zero_conv = jnp.zeros((DEPTH, nb, CONV_WIDTH - 1, 3 * DN_WIDTH), x_prompt.dtype)
    zero_dn = jnp.zeros((DEPTH, nb, DN_HEADS, DN_HEAD_DIM, DN_HEAD_DIM), jnp.float32)
    zero_gla = jnp.zeros((DEPTH, nb, GLA_HEADS, GLA_HEAD_DIM, GLA_HEAD_DIM), jnp.float32)
    y_prompt, conv_p, dn_p, gla_p = trunk(x_prompt, c_prompt, zero_conv, zero_dn, zero_gla, *params)
    y_sample, conv_s, dn_s, gla_s = trunk(x_sample, c_sample, state_dn_conv, state_dn, state_gla, *params)
    return (y_prompt, y_sample, conv_p, dn_p, gla_p, conv_s, dn_s, gla_s)
```

```python
import contextlib
import numpy as np
import concourse.bass as bass
import concourse.mybir as mybir
from concourse.bass_utils import run_bass_kernel_spmd

F32 = mybir.dt.float32
BF16 = mybir.dt.bfloat16
AF = mybir.ActivationFunctionType
ALU = mybir.AluOpType

ENGS = ("pe", "act", "dve", "pool", "sp")
N_DSEM = 72


class View:
    __slots__ = ("t", "ap")

    def __init__(self, t, ap):
        self.t = t
        self.ap = ap

    def __getitem__(self, idx):
        return View(self.t, self.ap[idx])

    def bc(self, shape):
        return View(self.t, self.ap.to_broadcast(list(shape)))

    def re(self, s, **kw):
        return View(self.t, self.ap.rearrange(s, **kw))

    def un(self, axis):
        return View(self.t, self.ap.unsqueeze(axis))


class Tile:
    def __init__(self, name, h):
        self.name = name
        self.h = h
        self.w = None
        self.rs = []
        self.dsem = None
        self.dcount = 0

    def __getitem__(self, idx):
        return View(self, self.h[idx])

    def bitcast(self, dt):
        return View(self, self.h.bitcast(dt)[:])


class Prog:
    def __init__(self, nc, es):
        self.nc = nc
        self.ges = es
        self.es = es
        self.q = {e: [] for e in ENGS}
        self.sem = {e: es.enter_context(nc.semaphore("cs_" + e)) for e in ENGS}
        self.cnt = {e: 0 for e in ENGS}
        self.waited = {e: {} for e in ENGS}
        self.out_events = []
        self.dpool = [[es.enter_context(nc.semaphore("ds%d" % i)), 0] for i in range(N_DSEM)]
        self.dfree = {"sw": list(range(0, 24)), "hw": list(range(24, N_DSEM))}
        self.dtiles = []
        self.ninstr = 0
        self.muted = False
        self.nphase = 0
        self.total = 0
        import os
        self.kops = int(os.environ.get("KOPS", "100000000"))
        self.kmark = bool(os.environ.get("KMARK"))

    def sb(self, name, shape, dt, persist=False):
        es = self.ges if persist else self.es
        return Tile(name, es.enter_context(self.nc.sbuf_tensor(name, list(shape), dt)))

    def ps(self, name, shape, dt=F32):
        return Tile(name, self.ges.enter_context(self.nc.psum_tensor(name, list(shape), dt)))

    def dram(self, name, shape, dt, kind=None):
        if kind is None:
            h = self.nc.dram_tensor(name, list(shape), dt)
        else:
            h = self.nc.dram_tensor(name, list(shape), dt, kind=kind)
        return Tile(name, h)

    def _deps(self, eng, reads, writes):
        evs = []
        for t in reads:
            if t.w is not None:
                evs.append(t.w)
        for t in writes:
            if t.w is not None:
                evs.append(t.w)
            for r in t.rs:
                evs.append(r)
        waits = []
        w = self.waited[eng]
        for (peng, sem, val) in evs:
            if peng == "pe" and eng == "pe":
                continue
            if w.get(sem.num, 0) >= val:
                continue
            w[sem.num] = val
            waits.append((sem, val))
        return waits

    def _commit(self, ev, reads, writes):
        for t in reads:
            t.rs.append(ev)
        for t in writes:
            t.w = ev
            t.rs = []

    def mark(self, name):
        if self.kmark:
            print("MARK", name, self.total)

    def op(self, eng, fn, reads=(), writes=()):
        if self.muted:
            return None
        self.total += 1
        if self.total > self.kops:
            return None
        reads = list(dict.fromkeys(reads))
        writes = list(dict.fromkeys(writes))
        waits = self._deps(eng, reads, writes)
        self.cnt[eng] += 1
        ev = (eng, self.sem[eng], self.cnt[eng])
        self.q[eng].append((waits, fn, (self.sem[eng], 1)))
        self._commit(ev, [t for t in reads if t not in writes], writes)
        return ev

    def dma(self, queue, out, in_, owner=None, is_output=False, **kw):
        if self.muted:
            return None
        self.total += 1
        if self.total > self.kops:
            return None
        reads = [in_.t]
        writes = [out.t]
        if owner is None:
            owner = in_.t if is_output else out.t
        qc = "sw" if queue == "pool" else "hw"
        if owner.dsem is None:
            owner.dsem = {}
        if qc not in owner.dsem:
            owner.dsem[qc] = self.dfree[qc].pop()
            self.dtiles.append((owner, qc))
        pe = self.dpool[owner.dsem[qc]]
        waits = self._deps(queue, reads, writes)
        pe[1] += 16
        ev = ("dma", pe[0], pe[1])
        oap, iap = out.ap, in_.ap
        sem = pe[0]
        self.q[queue].append((waits, lambda e: e.dma_start(out=oap, in_=iap, **kw), (sem, 16)))
        self._commit(ev, reads, writes)
        if is_output:
            self.out_events.append(ev)
        return ev

    def dump(self, name, view, shape, dt=F32):
        import os
        if not os.environ.get("KDEBUG"):
            return
        t = self.dram("dbg_" + name, shape, dt, kind="ExternalOutput")
        self.dma("sp", View(t, t.h[:]), view, is_output=True)

    def mm(self, out, lhsT, rhs, start=True, stop=True):
        o, l, r = out.ap, lhsT.ap, rhs.ap
        return self.op("pe", lambda e: e.matmul(o, lhsT=l, rhs=r, start=start, stop=stop),
                       [lhsT.t, rhs.t], [out.t])

    def tr(self, out, in_, ident):
        o, i, d = out.ap, in_.ap, ident.ap
        return self.op("pe", lambda e: e.transpose(o, i, d), [in_.t, ident.t], [out.t])

    def act(self, out, in_, func, bias=None, scale=None, accum=None):
        o, i = out.ap, in_.ap
        kw = {}
        reads = [in_.t]
        writes = [out.t]
        if bias is not None:
            if isinstance(bias, View):
                kw["bias"] = bias.ap
                reads.append(bias.t)
            else:
                kw["bias"] = bias
        if scale is not None:
            if isinstance(scale, View):
                kw["scale"] = scale.ap
                reads.append(scale.t)
            else:
                kw["scale"] = scale
        if accum is not None:
            kw["accum_out"] = accum.ap
            writes.append(accum.t)
        return self.op("act", lambda e: e.activation(out=o, in_=i, func=func, **kw), reads, writes)

    def tt(self, eng, out, a, b, op):
        o, x, y = out.ap, a.ap, b.ap
        return self.op(eng, lambda e: e.tensor_tensor(out=o, in0=x, in1=y, op=op), [a.t, b.t], [out.t])

    def ts(self, eng, out, in_, s1, op0, s2=None, op1=None):
        o, i = out.ap, in_.ap
        reads = [in_.t]
        a1, a2 = s1, s2
        if isinstance(s1, View):
            a1 = s1.ap
            reads.append(s1.t)
        if isinstance(s2, View):
            a2 = s2.ap
            reads.append(s2.t)
        if op1 is None:
            return self.op(eng, lambda e: e.tensor_scalar(out=o, in0=i, scalar1=a1, scalar2=None, op0=op0),
                           reads, [out.t])
        return self.op(eng, lambda e: e.tensor_scalar(out=o, in0=i, scalar1=a1, scalar2=a2, op0=op0, op1=op1),
                       reads, [out.t])

    def stt(self, out, in0, scalar, in1, op0, op1):
        o, x, y = out.ap, in0.ap, in1.ap
        reads = [in0.t, in1.t]
        sc = scalar
        if isinstance(scalar, View):
            sc = scalar.ap
            reads.append(scalar.t)
        return self.op("dve", lambda e: e.scalar_tensor_tensor(out=o, in0=x, scalar=sc, in1=y, op0=op0, op1=op1),
                       reads, [out.t])

    def copy(self, eng, out, in_):
        o, i = out.ap, in_.ap
        if eng == "act":
            return self.op("act", lambda e: e.activation(out=o, in_=i, func=AF.Copy), [in_.t], [out.t])
        return self.op(eng, lambda e: e.tensor_copy(out=o, in_=i), [in_.t], [out.t])

    def recip(self, out, in_):
        o, i = out.ap, in_.ap
        return self.op("dve", lambda e: e.reciprocal(out=o, in_=i), [in_.t], [out.t])

    def memset(self, eng, out, val):
        o = out.ap
        return self.op(eng, lambda e: e.memset(o, val), [], [out.t])

    def aselect(self, out, in_, pattern, cmp, fill, base, cm):
        o, i = out.ap, in_.ap
        return self.op("pool", lambda e: e.affine_select(out=o, in_=i, pattern=pattern, compare_op=cmp, fill=fill,
                                                         base=base, channel_multiplier=cm), [in_.t], [out.t])

    def emit_phase(self, final=False):
        import os
        if self.muted:
            return
        self.nphase += 1
        if self.nphase > int(os.environ.get("KSTOP", "99")):
            self.muted = True
        nc = self.nc
        q = self.q
        self.ninstr += sum(len(v) for v in q.values())
        if not hasattr(self, "newcnt"):
            self.newcnt = {e: 0 for e in ENGS}
            self.oldmap = {e: {0: 0} for e in ENGS}
        num2eng = {self.sem[e].num: e for e in ENGS}
        sig = {e: set() for e in ENGS}
        for name in ENGS:
            for (waits, fn, inc) in q[name]:
                for (sem, val) in waits:
                    e2 = num2eng.get(sem.num)
                    if e2 is not None:
                        sig[e2].add(val)
        for e in ENGS:
            if self.cnt[e] > 0:
                sig[e].add(self.cnt[e])
        for e in ENGS:
            for v in sorted(sig[e]):
                if v not in self.oldmap[e]:
                    self.newcnt[e] += 1
                    self.oldmap[e][v] = self.newcnt[e]
        omap = self.oldmap

        def mapw(sem, val):
            e2 = num2eng.get(sem.num)
            if e2 is None:
                return val
            return omap[e2][val]

        targets = [(self.sem[e], self.cnt[e]) for e in ENGS if self.cnt[e] > 0]
        targets += [(p[0], p[1]) for p in self.dpool if p[1] > 0]
        waited = self.waited
        base = {e: self.cnt[e] - sum(1 for it in q[e] if it[2][1] == 1) for e in ENGS}

        def run(name, e):
            c = base[name]
            for (waits, fn, inc) in q[name]:
                for (sem, val) in waits:
                    e.wait_ge(sem, mapw(sem, val))
                ins = fn(e)
                if inc[1] == 16:
                    ins.then_inc(inc[0], 16)
                else:
                    c += 1
                    if c in sig[name]:
                        ins.then_inc(inc[0], 1)
            for (sem, val) in targets:
                if waited[name].get(sem.num, 0) >= val:
                    continue
                waited[name][sem.num] = val
                e.wait_ge(sem, mapw(sem, val))

        with nc.Block() as block:
            @block.sync
            def _(e):
                run("sp", e)

            @block.tensor
            def _(e):
                run("pe", e)

            @block.scalar
            def _(e):
                run("act", e)

            @block.vector
            def _(e):
                run("dve", e)

            @block.gpsimd
            def _(e):
                run("pool", e)

        for k in q:
            q[k] = []
        for (t, qc) in self.dtiles:
            self.dfree[qc].append(t.dsem.pop(qc))
        self.dtiles = []


D = 1024
KD = 8
DFF = 4096
NCOL = 4120
EPS = 1e-6
LP = 2048
NS = 16
LS = 4
TS_P = 128
C_Q0, C_K0, C_V0, C_Z0, C_A0, C_B0 = 0, 512, 1024, 1536, 2048, 2052
C_GQ, C_GK, C_GV, C_GR, C_GG = 2056, 2568, 3080, 3592, 4104
DKS = 128.0 ** -0.5


class Grp:
    pass


def build(debug=False):
    nc = bass.Bass("TRN2", target_bir_lowering=False)
    ges = contextlib.ExitStack()
    with ges:
        P = Prog(nc, ges)
        I = lambda name, shape: P.dram(name, shape, F32, kind="ExternalInput")
        O = lambda name, shape: P.dram(name, shape, F32, kind="ExternalOutput")
        xp = I("xp", [LP, D]); xs = I("xs", [NS * LS, D])
        convs = I("convs", [NS * 3, 1536]); sdn = I("sdn", [NS, 4, 128, 128]); sgla = I("sgla", [NS, 4, 128, 128])
        cvec = I("cvec", [17, D])
        w_ada = I("w_ada", [D, 6 * D]); b_ada = I("b_ada", [6 * D]); w_norm1 = I("w_norm1", [D])
        w_in = I("w_in", [D, NCOL]); w_conv = I("w_conv", [4, 1536]); a_log = I("dn_a_log", [4])
        dt_bias = I("dn_dt_bias", [4]); w_dn_norm = I("w_dn_norm", [128]); w_g2 = I("w_gla_g2", [16, 512])
        b_g = I("b_gla_g", [512]); w_gla_norm = I("w_gla_norm", [128]); w_o = I("w_o", [D, D])
        w_norm2 = I("w_norm2", [D]); w_up = I("w_up", [D, DFF]); w_down = I("w_down", [DFF, D])
        w_norm_f = I("w_norm_f", [D])
        y_p = O("y_p", [LP, D]); y_s = O("y_s", [NS * LS, D])
        conv_p = O("conv_p", [3, 1536]); dn_p = O("dn_p", [4, 128, 128]); gla_p = O("gla_p", [4, 128, 128])
        conv_s = O("conv_s", [NS * 3, 1536]); dn_s = O("dn_s", [NS, 4, 128, 128]); gla_s = O("gla_s", [NS, 4, 128, 128])
        x1d_p = P.dram("x1d_p", [LP, D], F32)
        x1d_s = P.dram("x1d_s", [NS * LS, D], F32)
        dbg = {}

        def bcast_rows(t, n, width, off=0):
            return View(t, bass.AP(t.h, off, [[0, n], [1, width]]))

        B = [P.ps("bank%d" % i, [128, 512]) for i in range(8)]
        rr = {"tr": [0, [0, 1]], "mm": [0, [0, 1]], "rc": [0, [4, 5]], "mmB": [0, [2, 3]]}

        def bank(pool):
            st = rr[pool]
            b = B[st[1][st[0] % len(st[1])]]
            st[0] += 1
            return b

        ident_f = P.sb("ident_f", [128, 128], F32, persist=True)
        ones_f = P.sb("ones_f", [128, 128], F32, persist=True)
        ones_b = P.sb("ones_b", [128, 128], BF16, persist=True)
        P.memset("pool", ones_f[:], 1.0)
        P.memset("pool", ones_b[:], 1.0)
        P.aselect(ident_f[:], ones_f[:], [[-1, 128]], ALU.is_equal, 0.0, 0, 1)
        ident_b = P.sb("ident_b", [128, 128], BF16, persist=True)
        P.copy("pool", ident_b[:], ident_f[:])

        def make_masks(g, C, bs):
            g.UI = P.sb(g.name + "UI", [C, C], F32, persist=True)
            g.US = P.sb(g.name + "US", [C, C], F32, persist=True)
            g.LSm = P.sb(g.name + "LS", [C, C], F32, persist=True)
            P.aselect(g.UI[:], ones_f[0:C, 0:C], [[1, C]], ALU.is_ge, 0.0, 0, -1)
            P.aselect(g.US[:], ones_f[0:C, 0:C], [[1, C]], ALU.is_gt, 0.0, 0, -1)
            P.aselect(g.LSm[:], ones_f[0:C, 0:C], [[-1, C]], ALU.is_gt, 0.0, 0, 1)
            if bs < C:
                nb = C // bs
                for m in (g.UI, g.US, g.LSm):
                    v = m[:].re("p (s t) -> p s t", t=bs)
                    P.aselect(v, v, [[-bs, nb], [0, bs]], ALU.is_ge, 0.0, 0, 1)
                    P.aselect(v, v, [[bs, nb], [0, bs]], ALU.is_ge, 0.0, bs - 1, -1)

        GP = Grp(); GP.name = "p"; GP.C = 128; GP.TS = TS_P; GP.nseq = 1; GP.T = TS_P; GP.nst = LP // TS_P
        GS = Grp(); GS.name = "s"; GS.C = 64; GS.TS = 64; GS.nseq = NS; GS.T = LS; GS.nst = 1
        GP.x = xp; GS.x = xs; GP.x1d = x1d_p; GS.x1d = x1d_s; GP.y = y_p; GS.y = y_s
        GP.v0 = 0; GS.v0 = 1
        GP.tok0 = 0; GS.tok0 = LP
        make_masks(GP, 128, 128)
        make_masks(GS, 64, 4)
        NC_SLOW = dict(allow_slow_non_contiguous=True)
        b_adaT = P.sb("b_adaT", [128, 48], F32, persist=True)
        P.dma("sp", b_adaT[:], View(b_ada, b_ada.h.rearrange("(c p) -> p c", p=128)), **NC_SLOW)
        wn1T = P.sb("wn1T", [128, 8], F32, persist=True)
        wn2T = P.sb("wn2T", [128, 8], F32, persist=True)
        P.dma("sp", wn1T[:], View(w_norm1, w_norm1.h.rearrange("(c p) -> p c", p=128)), **NC_SLOW)
        P.dma("sp", wn2T[:], View(w_norm2, w_norm2.h.rearrange("(c p) -> p c", p=128)), **NC_SLOW)
        wconvT = P.sb("wconvT", [128, 4, 12], F32, persist=True)
        for i in range(4):
            P.dma("sp", wconvT[:, i, :], View(w_conv, w_conv.h[i, :].rearrange("(c p) -> p c", p=128)), **NC_SLOW)
        wdnn = P.sb("wdnn", [128, 1], F32, persist=True)
        wglan = P.sb("wglan", [128, 1], F32, persist=True)
        P.dma("sp", wdnn[:], View(w_dn_norm, w_dn_norm.h.rearrange("(p o) -> p o", o=1)))
        P.dma("sp", wglan[:], View(w_gla_norm, w_gla_norm.h.rearrange("(p o) -> p o", o=1)))
        alog_bc = P.sb("alog_bc", [128, 4], F32, persist=True)
        dtb_bc = P.sb("dtb_bc", [128, 4], F32, persist=True)
        P.dma("sp", alog_bc[:], bcast_rows(a_log, 128, 4))
        P.dma("sp", dtb_bc[:], bcast_rows(dt_bias, 128, 4))
        nexpa = P.sb("nexpa", [128, 4], F32, persist=True)
        P.act(nexpa[:], alog_bc[:], AF.Exp)
        P.ts("dve", nexpa[:], nexpa[:], -1.0, ALU.mult)
        wg2_b = P.sb("wg2_b", [16, 512], BF16, persist=True)
        bg_b = P.sb("bg_b", [1, 512], BF16, persist=True)
        P.dma("pool", wg2_b[:], w_g2[:])
        P.dma("pool", bg_b[:], View(b_g, b_g.h.rearrange("(o c) -> o c", o=1)))
        bgt_b = P.sb("bgt_b", [1, 2, 1024], F32, persist=True)
        P.dma("sp", bgt_b[:, 0, :], View(b_ada, b_ada.h.rearrange("(o c) -> o c", o=1)[:, 2048:3072]))
        P.dma("sp", bgt_b[:, 1, :], View(b_ada, b_ada.h.rearrange("(o c) -> o c", o=1)[:, 5120:6144]))

        modT = P.sb("modT", [128, 48, 17], F32, persist=True)
        sc1T = P.sb("sc1T", [128, 8, 17], F32, persist=True)
        sc2T = P.sb("sc2T", [128, 8, 17], F32, persist=True)
        GP.gtok = P.sb("gtok_p", [128, 2, 1024], F32, persist=True)
        GS.gtok = P.sb("gtok_s", [64, 2, 1024], F32, persist=True)

        with contextlib.ExitStack() as pes:
            P.es = pes
            c_sb = P.sb("c_sb", [17, D], F32)
            P.dma("sp", c_sb[:], cvec[:])
            P.act(c_sb[:], c_sb[:], AF.Silu)
            cT = P.sb("cT", [128, 8, 17], F32)
            for k in range(8):
                pb = bank("tr")
                P.tr(pb[:, 0:17], c_sb[0:17, k * 128:(k + 1) * 128], ident_f[0:17, 0:17])
                P.copy("dve", cT[:, k, :], pb[:, 0:17])
            cTe_p = P.sb("cTe_p", [128, 8, 128], F32)
            cTe_s = P.sb("cTe_s", [128, 8, 64], F32)
            P.copy("dve", cTe_p[:], cT[:, :, 0:1].bc([128, 8, 128]))
            for k in range(8):
                P.copy("dve", cTe_s[:, k, :].re("p (s t) -> p s t", t=4), cT[:, k, 1:17].un(2).bc([128, 16, 4]))
            wa = [P.sb("wa%d" % i, [128, 8, 512], F32) for i in range(2)]
            wab = [P.sb("wab%d" % i, [128, 8, 512], BF16) for i in range(3)]
            cT_b = P.sb("cT_b", [128, 8, 17], BF16)
            cTe_pb = P.sb("cTe_pb", [128, 8, 128], BF16)
            cTe_sb = P.sb("cTe_sb", [128, 8, 64], BF16)
            bgt_bb = P.sb("bgt_bb", [1, 2, 1024], BF16)
            P.copy("dve", cT_b[:], cT[:])
            P.copy("dve", cTe_pb[:], cTe_p[:])
            P.copy("dve", cTe_sb[:], cTe_s[:])
            P.copy("dve", bgt_bb[:], bgt_b[:])
            for gi in range(12):
                hi_prec = gi < 4
                if hi_prec:
                    wb = wa[gi % 2]
                    P.dma("sp", wb[:], View(w_ada, w_ada.h.rearrange("(k p) c -> p k c", p=128)[:, :, gi * 512:(gi + 1) * 512]))
                    cTx = cT
                else:
                    wb = wab[gi % 3]
                    P.dma("pool", wb[:], View(w_ada, w_ada.h.rearrange("(k p) c -> p k c", p=128)[:, :, gi * 512:(gi + 1) * 512]))
                    cTx = cT_b
                for j in range(4):
                    ch = gi * 4 + j
                    pb = bank("mm")
                    for k in range(8):
                        P.mm(pb[:, 0:17], wb[:, k, j * 128:(j + 1) * 128], cTx[:, k, :], start=(k == 0), stop=(k == 7))
                    P.act(modT[:, ch, :], pb[:, 0:17], AF.Identity, bias=b_adaT[:, ch:ch + 1])
                if gi in (4, 5, 10, 11):
                    which = 0 if gi < 6 else 1
                    half = gi % 2
                    for (g, cte) in ((GP, cTe_pb), (GS, cTe_sb)):
                        pb = bank("mm")
                        for k in range(8):
                            P.mm(pb[0:g.C, :], cte[:, k, :], wb[:, k, :], start=(k == 0), stop=False)
                        P.mm(pb[0:g.C, :], ones_b[0:1, 0:g.C], bgt_bb[0:1, which, half * 512:(half + 1) * 512],
                             start=False, stop=True)
                        P.copy("act", g.gtok[:, which, half * 512:(half + 1) * 512], pb[0:g.C, :])
            for k in range(8):
                P.stt(sc1T[:, k, :], modT[:, 8 + k, :], 1.0, wn1T[:, k:k + 1].bc([128, 17]), ALU.add, ALU.mult)
                P.stt(sc2T[:, k, :], modT[:, 32 + k, :], 1.0, wn2T[:, k:k + 1].bc([128, 17]), ALU.add, ALU.mult)
            P.dump("modT", modT[:], [128, 48, 17])
            P.dump("gtok_p", GP.gtok[:], [128, 2, 1024])
            P.dump("gtok_s", GS.gtok[:], [64, 2, 1024])
            P.dump("sc1T", sc1T[:], [128, 8, 17])
            P.emit_phase()
        P.es = ges

        def load_x(g, st, xt, src=None):
            src = g.x if src is None else src
            r0 = st * g.TS
            nch = g.TS // g.C
            P.dma("sp", xt[0:g.C, 0:nch, :], View(src, src.h[r0:r0 + g.TS, :].rearrange("(c p) d -> p c d", p=g.C)))

        def norm_to_T(g, xin, hT, col0, scT, shbase, tmp, hT32=None):
            C = g.C
            junk, ss, sd, rstd, xsn = tmp[0:5]
            P.act(junk[0:C, :], xin, AF.Square, accum=ss[0:C, :])
            P.act(sd[0:C, :], ss[0:C, :], AF.Ln, bias=EPS, scale=1.0 / D)
            P.act(rstd[0:C, :], sd[0:C, :], AF.Exp, scale=-0.5)
            P.act(xsn[0:C, :], xin, AF.Identity, scale=rstd[0:C, :])
            for half in range(2):
                pb = bank("tr")
                for kk in range(4):
                    k = half * 4 + kk
                    P.tr(pb[:, kk * C:(kk + 1) * C], xsn[0:C, k * 128:(k + 1) * 128], ident_f[0:C, 0:C])
                for kk in range(4):
                    k = half * 4 + kk
                    src = pb[:, kk * C:(kk + 1) * C]
                    dstb = hT[k][:, col0:col0 + C]
                    dst = dstb if hT32 is None else hT32[k][:, col0:col0 + C]
                    if g.nseq == 1:
                        P.act(dst, src, AF.Identity, bias=modT[:, shbase + k, 0:1], scale=scT[:, k, 0:1])
                    else:
                        t2 = tmp[5]
                        P.tt("dve", t2[:, 0:C].re("p (s t) -> p s t", t=LS), src.re("p (s t) -> p s t", t=LS),
                             scT[:, k, 1:17].un(2).bc([128, NS, LS]), ALU.mult)
                        P.tt("dve", dst.re("p (s t) -> p s t", t=LS), t2[:, 0:C].re("p (s t) -> p s t", t=LS),
                             modT[:, shbase + k, 1:17].un(2).bc([128, NS, LS]), ALU.add)
                    if hT32 is not None:
                        P.copy("pool", dstb, dst)

        def norm_tmps(pfx):
            return (P.sb(pfx + "junk", [128, D], BF16), P.sb(pfx + "ss", [128, 1], F32), P.sb(pfx + "sd", [128, 1], F32),
                    P.sb(pfx + "rstd", [128, 1], F32), P.sb(pfx + "xsn", [128, D], F32), P.sb(pfx + "t2", [128, 128], F32))

        def head_norm_gen(g, ob, gate_v, wn, out_v, tmps, via_act=False):
            C = g.C
            W = 4 * C
            sq, sd, rs, on = tmps
            P.act(sq[:, 0:W], ob[:, 0:W], AF.Square)
            if via_act:
                P.copy("act", on[:, 0:W], ob[:, 0:W])
            yield
            pb = bank("mm")
            P.mm(pb[:, 0:W], ones_b[:, :], sq[:, 0:W])
            yield
            P.act(sd[:, 0:W], pb[:, 0:W], AF.Ln, bias=EPS, scale=1.0 / 128)
            P.act(rs[:, 0:W], sd[:, 0:W], AF.Exp, scale=-0.5)
            yield
            if via_act:
                P.tt("dve", on[:, 0:W], on[:, 0:W], rs[:, 0:W], ALU.mult)
            else:
                P.tt("dve", on[:, 0:W], ob[:, 0:W], rs[:, 0:W], ALU.mult)
            P.stt(out_v, on[:, 0:W].re("p (h c) -> p h c", c=C), wn[:, 0:1], gate_v, ALU.mult, ALU.mult)
            yield

        def head_norm(g, ob, gate_v, wn, out_v, tmps):
            for _ in head_norm_gen(g, ob, gate_v, wn, out_v, tmps):
                pass


        abes = contextlib.ExitStack()
        abes.__enter__()
        P.es = abes
        BMf = P.sb("BMf", [128, NS, 64], BF16)
        BMp = P.sb("BMp", [64, NS, 128], BF16)
        omix_dn = P.sb("omix_dn", [128, 4, LP + NS * LS], BF16)
        P.es = ges
        P.memset("pool", BMf[:], 1.0)
        P.memset("pool", BMp[:], 1.0)
        P.aselect(BMf[:], BMf[:], [[-4, NS], [1, 64]], ALU.is_ge, 0.0, 0, 0)
        P.aselect(BMf[:], BMf[:], [[4, NS], [-1, 64]], ALU.is_ge, 0.0, 3, 0)
        P.aselect(BMp[:], BMp[:], [[-4, NS], [0, 128]], ALU.is_ge, 0.0, 0, 1)
        P.aselect(BMp[:], BMp[:], [[4, NS], [0, 128]], ALU.is_ge, 0.0, 3, -1)

        with contextlib.ExitStack() as pes:
            P.es = pes
            rr["mm"][1] = [0, 1]
            NA = 2056
            wA = P.sb("w_inA", [128, 8, NA], BF16)
            for k2 in range(2):
                P.dma("pool", wA[:, 4 * k2:4 * k2 + 4, :],
                      View(w_in, w_in.h.rearrange("(k p) c -> p k c", p=128)[:, 4 * k2:4 * k2 + 4, 0:NA]))
            ntm = norm_tmps("A")
            xt = P.sb("A_xt", [128, 2, D], F32)
            hT = [P.sb("A_hT%d" % i, [128, TS_P], BF16) for i in range(8)]
            hT32 = [P.sb("A_hT32_%d" % i, [128, TS_P], F32) for i in range(8)]
            wab32 = P.sb("A_wab32", [128, 8, 8], F32)
            for k in range(8):
                P.dma("sp", wab32[:, k, :], w_in[k * 128:(k + 1) * 128, C_A0:C_A0 + 8], **NC_SLOW)
            hsave = P.sb("A_hsave", [128, 12, 3], F32)
            qkvs = P.sb("A_qkvs", [128, 12, TS_P], F32)
            sqb = ntm[0][:, 0:8 * TS_P].re("p (c t) -> p c t", t=TS_P)
            cv_in = qkvs[0:48, :, :].re("p a b -> p (a b)")
            cv_out = cv_in
            cvst = ntm[4][:, 0:576].re("p (c j) -> p c j", j=48)
            qkb2 = [P.sb("A_qkb%d" % i, [128, 8, TS_P], BF16) for i in range(2)]
            zs2 = [P.sb("A_zs%d" % i, [128, 4, TS_P], BF16) for i in range(2)]
            ab = P.sb("A_ab", [128, 8], F32)
            beta2 = [P.sb("A_beta%d" % i, [128, 4], F32) for i in range(2)]
            nbeta2 = [P.sb("A_nbeta%d" % i, [128, 4], F32) for i in range(2)]
            gg_ = P.sb("A_g", [128, 4], F32)
            tmp4 = P.sb("A_tmp4", [128, 4], F32)
            gU = P.sb("A_gU", [128, 4, 128], F32)
            eGrow2 = [P.sb("A_eGrow%d" % i, [128, 4, 128], F32) for i in range(2)]
            nGcol2 = [P.sb("A_nGcol%d" % i, [128, 4], F32) for i in range(2)]
            eGL = P.sb("A_eGL", [128, 4], F32)
            hn = (P.sb("A_hsq", [128, 512], BF16), P.sb("A_hsd", [128, 512], F32), P.sb("A_hrs", [128, 512], F32),
                  P.sb("A_hon", [128, 512], F32))
            n_sd = hn[1]
            n_rs = hn[2]
            NR = 4

            def rot(name, shape, dt):
                return [P.sb("A_%s%d" % (name, i), shape, dt) for i in range(NR)]
            xcl = [P.sb("A_xcl%d" % i, [128, 128], F32) for i in range(2)]
            ETs2 = [[P.sb("A_ETs%d_%d" % (p_, h_), [128, 128], BF16) for h_ in range(4)] for p_ in range(2)]
            ETi2 = [[P.sb("A_ETi%d_%d" % (p_, h_), [128, 128], BF16) for h_ in range(4)] for p_ in range(2)]
            ELb2 = [[P.sb("A_ELb%d_%d" % (p_, h_), [128, 128], BF16) for h_ in range(4)] for p_ in range(2)]
            nLS = {}
            for g_ in (GP, GS):
                nLS[g_.name] = P.sb("A_nLS" + g_.name, [g_.C, g_.C], F32)
                P.ts("pool", nLS[g_.name][:], g_.LSm[:], -1.0, ALU.mult)
            Mp = rot("Mp", [128, 128], BF16); Np = rot("Np", [128, 128], BF16)
            QA = rot("QA", [128, 128], BF16); QB = rot("QB", [128, 128], BF16)
            QtY = rot("QtY", [128, 256], BF16); QtYb = rot("QtYb", [128, 256], BF16)
            attnT = rot("attnT", [128, 128], BF16)
            N0 = rot("N0", [128, 128], BF16); Xc = rot("Xc", [128, 128], BF16); Xn = rot("Xn", [128, 128], BF16)
            E1 = rot("E1", [128, 128], BF16); Et1 = rot("Et1", [128, 128], BF16); Zs = rot("Zs", [128, 256], BF16)
            Et2 = rot("Et2", [128, 128], BF16)
            Gsb2 = [P.sb("A_Gsb%d" % i, [128, 512], F32) for i in range(2)]
            bd32 = P.sb("A_bd32", [128, 128], BF16); m64 = P.sb("A_m64", [128, 128], BF16); m128 = P.sb("A_m128", [128, 128], BF16)
            bd64 = P.sb("A_bd64", [128, 128], BF16)
            for (mt, bs_) in ((bd32, 32), (bd64, 64)):
                P.memset("pool", mt[:], 1.0)
                v_ = mt[:].re("p (s t) -> p s t", t=bs_)
                P.aselect(v_, v_, [[-bs_, 128 // bs_], [0, bs_]], ALU.is_ge, 0.0, 0, 1)
                P.aselect(v_, v_, [[bs_, 128 // bs_], [0, bs_]], ALU.is_ge, 0.0, bs_ - 1, -1)
            P.tt("pool", m64[:], bd64[:], bd32[:], ALU.subtract)
            P.ts("pool", m128[:], bd64[:], -1.0, ALU.mult, 1.0, ALU.add)
            qdT2 = [rot("qdT%d_" % i, [128, 128], BF16) for i in range(2)]; kgT2 = [rot("kgT%d_" % i, [128, 128], BF16) for i in range(2)]
            kdec2 = [rot("kdec%d_" % i, [128, 128], BF16) for i in range(2)]; vsb2 = [rot("vsb%d_" % i, [128, 128], F32) for i in range(2)]
            Rp = rot("Rp", [128, 128], BF16); u_ = rot("u", [128, 128], BF16)
            Sf_p = [P.sb("A_Sf_p%d" % i, [128, 128], F32) for i in range(4)]
            Sb_p = [P.sb("A_Sb_p%d" % i, [128, 128], BF16) for i in range(4)]
            for i in range(4):
                P.memset("pool", Sf_p[i][:], 0.0)
                P.memset("pool", Sb_p[i][:], 0.0)
            Sf_s = P.sb("A_Sf_s", [128, NS, 128], F32)
            Sb_s = P.sb("A_Sb_s", [128, NS, 128], BF16)
            P.memset("pool", Sf_s[:], 0.0)
            kgT_m = P.sb("A_kgTm", [128, NS, 64], BF16)
            qdT_m = P.sb("A_qdTm", [128, NS, 64], BF16)
            kdec_m = P.sb("A_kdecm", [64, NS, 128], BF16)

            cb_all = P.sb("A_cb", [128, 12, 3 + TS_P], F32)
            stages = [(g_, st_) for g_ in (GP, GS) for st_ in range(g_.nst)]

            def front(g, st, par):
                C, TS, nseq, T = g.C, g.TS, g.nseq, g.T
                c = 0
                tsl = slice(0, C)
                qkb, zs, eGrow, beta, nbeta, Gsb, nGcol = qkb2[par], zs2[par], eGrow2[par], beta2[par], nbeta2[par], Gsb2[par], nGcol2[par]
                qdT, kgT, kdec, vsb = qdT2[par], kgT2[par], kdec2[par], vsb2[par]
                cb = cb_all[:, :, 0:nseq * (3 + T)].re("p c (s j) -> p c s j", j=3 + T)
                if st == 0:
                    cb = cb_all[:, :, 0:nseq * (3 + T)].re("p c (s j) -> p c s j", j=3 + T)
                    if nseq == 1:
                        P.memset("pool", cb[:, :, :, 0:3], 0.0)
                    else:
                        P.dma("sp", cv_in[:], convs[:])
                        for cc in range(12):
                            pb = bank("tr")
                            P.tr(pb[:, 0:48], cv_in[0:48, cc * 128:(cc + 1) * 128], ident_f[0:48, 0:48])
                            P.copy("act", cb[:, cc, :, 0:3], pb[:, 0:48].re("p (s j) -> p s j", j=3))
                    yield
                load_x(g, st, xt)
                norm_to_T(g, xt[0:C, 0, :], hT, 0, sc1T, 0, ntm, hT32=hT32)
                yield
                P.mark("A %s st%d normT done" % (g.name, st))
                if nseq == 1 and st > 0:
                    P.copy("pool", cb[:, :, 0, 0:3], hsave[:])
                for cc in range(16):
                    pb = bank("mm")
                    for k in range(8):
                        P.mm(pb[:, 0:TS], wA[:, k, cc * 128:(cc + 1) * 128], hT[k][:, 0:TS], start=(k == 0), stop=(k == 7))
                    if cc < 12:
                        P.copy("act", cb[:, cc, :, 3:3 + T], pb[:, 0:TS].re("p (s t) -> p s t", t=T))
                    else:
                        P.act(zs[:, cc - 12, 0:TS], pb[:, 0:TS], AF.Silu)
                    if cc % 2 == 1:
                        yield
                if nseq == 1:
                    P.copy("pool", hsave[:], cb[:, :, 0, T:T + 3])
                P.mark("A %s st%d inproj done" % (g.name, st))
                if st == g.nst - 1:
                    nr = nseq * 3
                    for cc in range(12):
                        P.copy("pool", cvst[:, cc, 0:nr].re("p (s j) -> p s j", j=3), cb[:, cc, :, T:T + 3])
                        pb = bank("tr")
                        P.tr(pb[0:nr, 0:128], cvst[:, cc, 0:nr], ident_f[:, :])
                        P.copy("act", cv_out[0:nr, cc * 128:(cc + 1) * 128], pb[0:nr, 0:128])
                    if nseq == 1:
                        P.dma("sp", conv_p[:], cv_out[0:3, :], is_output=True)
                    else:
                        P.dma("sp", conv_s[:], cv_out[0:48, :], is_output=True)
                yield
                for cc in range(12):
                    av = qkvs[:, cc, 0:TS].re("p (s t) -> p s t", t=T)
                    P.ts("dve", av, cb[:, cc, :, 0:T], wconvT[:, 0, cc:cc + 1], ALU.mult)
                    for i in range(1, 4):
                        P.stt(av, cb[:, cc, :, i:i + T], wconvT[:, i, cc:cc + 1], av, ALU.mult, ALU.add)
                    if cc % 2 == 1:
                        yield
                P.act(qkvs[:, :, 0:TS], qkvs[:, :, 0:TS], AF.Silu)
                P.mark("A %s st%d conv done" % (g.name, st))
                P.act(sqb[:, :, 0:TS], qkvs[:, 0:8, 0:TS], AF.Square)
                for half in range(2):
                    pb = bank("mm")
                    for j in range(4):
                        P.mm(pb[:, j * TS:(j + 1) * TS], ones_b[:, :], sqb[:, half * 4 + j, 0:TS])
                    W4 = 4 * TS
                    P.act(n_sd[:, 0:W4], pb[:, 0:W4], AF.Ln, bias=EPS)
                    P.act(n_rs[:, 0:W4], n_sd[:, 0:W4], AF.Exp, scale=-0.5)
                    qv = qkvs[:, half * 4:(half + 1) * 4, 0:TS]
                    P.stt(qv, qv, DKS if half == 0 else 1.0, n_rs[:, 0:W4].re("p (c t) -> p c t", t=TS), ALU.mult, ALU.mult)
                    yield
                P.copy("act", qkb[:, :, 0:TS], qkvs[:, 0:8, 0:TS])
                if nseq == 1 and st == 0:
                    P.dump("qkvs", qkvs[:], [128, 12, TS_P])
                P.mark("A %s st%d l2norm done" % (g.name, st))
                pb = bank("mm")
                for k in range(8):
                    P.mm(pb[0:C, 0:8], hT32[k][:, tsl], wab32[:, k, :], start=(k == 0), stop=(k == 7))
                P.copy("act", ab[0:C, :], pb[0:C, 0:8])
                P.act(beta[0:C, :], ab[0:C, 4:8], AF.Sigmoid)
                P.ts("dve", nbeta[0:C, :], beta[0:C, :], -1.0, ALU.mult)
                P.tt("dve", tmp4[0:C, :], ab[0:C, 0:4], dtb_bc[0:C, :], ALU.add)
                P.act(tmp4[0:C, :], tmp4[0:C, :], AF.Exp)
                P.act(tmp4[0:C, :], tmp4[0:C, :], AF.Ln, bias=1.0)
                P.tt("dve", gg_[0:C, :], tmp4[0:C, :], nexpa[0:C, :], ALU.mult)
                P.tt("dve", gU[0:C, :, 0:C], g.UI[:].un(1).bc([C, 4, C]), gg_[0:C, :].un(2).bc([C, 4, C]), ALU.mult)
                yield
                Gb = B[7]
                for h in range(4):
                    P.mm(Gb[:, h * C:(h + 1) * C], ones_f[0:C, :], gU[0:C, h, 0:C])
                pb = bank("tr")
                P.mm(pb[0:C, 0:4], g.UI[:], gg_[0:C, :])
                P.mm(pb[0:C, 8:12], g.LSm[:], gg_[0:C, :])
                P.ts("dve", nGcol[0:C, :], pb[0:C, 0:4], -1.0, ALU.mult)
                P.act(eGL[0:C, :], pb[0:C, 8:12], AF.Exp)
                P.act(eGrow[:, :, 0:C], Gb[:, 0:4 * C].re("p (h c) -> p h c", c=C), AF.Exp)
                P.copy("act", Gsb[0:C, 0:4 * C], Gb[0:C, 0:4 * C])
                for h in range(4):
                    xc = xcl[h % 2]
                    P.stt(xc[0:C, 0:C], Gsb[0:C, h * C:(h + 1) * C], nGcol[0:C, h:h + 1], g.UI[:], ALU.add, ALU.mult)
                    P.act(xc[0:C, 0:C], xc[0:C, 0:C], AF.Exp)
                    P.tt("pool", ETs2[par][h][0:C, 0:C], xc[0:C, 0:C], g.US[:], ALU.mult)
                    P.tt("pool", ETi2[par][h][0:C, 0:C], xc[0:C, 0:C], g.UI[:], ALU.mult)
                    yield
                P.tt("dve", gU[0:C, :, 0:C], ident_f[0:C, 0:C].un(1).bc([C, 4, C]), beta[0:C, :].un(2).bc([C, 4, C]), ALU.mult)
                Bb = bank("mm")
                for h in range(4):
                    P.mm(Bb[:, h * C:(h + 1) * C], ones_f[0:C, :], gU[0:C, h, 0:C])
                for h in range(4):
                    xc = xcl[h % 2]
                    P.stt(xc[0:C, 0:C], Gsb[0:C, h * C:(h + 1) * C], nGcol[0:C, h:h + 1], nLS[g.name][:], ALU.add, ALU.mult)
                    P.act(xc[0:C, 0:C], xc[0:C, 0:C], AF.Exp)
                    P.tt("pool", xc[0:C, 0:C], xc[0:C, 0:C], g.LSm[:], ALU.mult)
                    P.stt(ELb2[par][h][0:C, 0:C], xc[0:C, 0:C], -1.0, Bb[0:C, h * C:(h + 1) * C], ALU.mult, ALU.mult)
                yield
                yield
                for h in range(4):
                    r = h
                    P.tt("pool", qdT[r][:, 0:C], qkvs[:, h, tsl], eGrow[:, h, 0:C], ALU.mult)
                    P.tt("pool", kgT[r][:, 0:C], qkvs[:, 4 + h, tsl], eGrow[:, h, 0:C], ALU.mult)
                    pv = bank("tr")
                    P.tr(pv[0:C, 0:128], qkvs[:, 4 + h, tsl], ident_f[:, :])
                    P.tr(pv[0:C, 128:256], qkvs[:, 8 + h, tsl], ident_f[:, :])
                    P.act(kdec[r][0:C, :], pv[0:C, 0:128], AF.Identity, scale=eGL[0:C, h:h + 1])
                    P.copy("act", vsb[r][0:C, :], pv[0:C, 128:256])
                    yield
                yield

            def heads(g, st, par, nxt):
                C, TS, nseq, T = g.C, g.TS, g.nseq, g.T
                tsl = slice(0, C)
                gtok0 = g.tok0 + st * TS
                qkb, zs, eGrow, beta, nbeta, Gsb, nGcol = qkb2[par], zs2[par], eGrow2[par], beta2[par], nbeta2[par], Gsb2[par], nGcol2[par]
                qdT, kgT, kdec, vsb = qdT2[par], kgT2[par], kdec2[par], vsb2[par]
                ob = B[6]
                def dn_head(h):
                    r = h
                    hb = B[2 + h]
                    kTb = qkb[:, 4 + h, tsl]
                    qTb = qkb[:, h, tsl]
                    P.mm(hb[0:C, 0:C], kTb, kTb)
                    P.mm(hb[0:C, 128:128 + C], kTb, qTb)
                    yield
                    P.stt(Mp[r][0:C, 0:C], hb[0:C, 0:C], nbeta[0:C, h:h + 1], ETs2[par][h][0:C, 0:C], ALU.mult, ALU.mult)
                    P.tt("dve", Np[r][0:C, 0:C], hb[0:C, 0:C], ELb2[par][h][0:C, 0:C], ALU.mult)
                    P.tt("dve", attnT[r][0:C, 0:C], hb[0:C, 128:128 + C], ETi2[par][h][0:C, 0:C], ALU.mult)
                    yield
                    hbb = hb.bitcast(BF16)

                    def neumann(Qv, QtYc, QtYn, nlev):
                        Q = Qv
                        cands = [QA[r], QB[r]]
                        for lv in range(nlev):
                            last = (lv == nlev - 1)
                            if last:
                                P.mm(hb[0:C, 128:128 + C], Q[0:C, 0:C], QtYc[0:C, 128:128 + C])
                            else:
                                P.mm(hb[0:C, 0:C], Q[0:C, 0:C], QtYc[0:C, 0:C])
                                P.mm(hb[0:C, 128:128 + C], Q[0:C, 0:C], QtYc[0:C, 128:128 + C])
                                P.mm(hb[0:C, 256:256 + C], QtYc[0:C, 0:C], Q[0:C, 0:C])
                            yield
                            P.tt("dve", QtYn[0:C, 128:128 + C], hb[0:C, 128:128 + C], QtYc[0:C, 128:128 + C], ALU.add)
                            if not last:
                                P.copy("act", QtYn[0:C, 0:C], hb[0:C, 0:C])
                                Qn = cands[lv % 2]
                                P.copy("dve", Qn[0:C, 0:C], hb[0:C, 256:256 + C])
                                Q = Qn
                            yield
                            QtYc, QtYn = QtYn, QtYc
                        res_[0] = (QtYc, QtYn)
                    res_ = [None]
                    QtYc = QtY[r]; QtYn = QtYb[r]
                    P.copy("pool", QtYc[0:C, 128:128 + C], ident_b[0:C, 0:C])
                    if C == 128:
                        P.tt("pool", QtYc[:, 0:128], Mp[r][:, :], bd32[:], ALU.mult)
                        P.tt("pool", N0[r][:, :], Np[r][:, :], bd32[:], ALU.mult)
                        P.tt("pool", E1[r][:, :], Mp[r][:, :], m64[:], ALU.mult)
                        P.tt("pool", Et1[r][:, :], Np[r][:, :], m64[:], ALU.mult)
                        P.tt("pool", Et2[r][:, :], Np[r][:, :], m128[:], ALU.mult)
                        yield
                        yield from neumann(N0[r], QtYc, QtYn, 5)
                        QtYc, QtYn = res_[0]
                        Ycur = QtYc[:, 128:256]
                        P.tr(hbb[:, 0:128], Ycur, ident_b[:, :])
                        yield
                        P.copy("act", Xc[r][:, :], hbb[:, 0:128])
                        yield
                        P.mm(hb[:, 128:256], E1[r][:, :], Xc[r][:, :])
                        P.mm(hb[:, 256:384], Et1[r][:, :], Ycur)
                        yield
                        P.copy("act", Zs[r][:, 0:256], hb[:, 128:384])
                        yield
                        P.mm(hb[:, 0:128], Ycur, Zs[r][:, 0:128])
                        P.mm(hb[:, 128:256], Xc[r][:, :], Zs[r][:, 128:256])
                        yield
                        P.tt("dve", Xn[r][:, :], hb[:, 0:128], Xc[r][:, :], ALU.add)
                        P.tt("dve", QtYn[:, 128:256], hb[:, 128:256], Ycur, ALU.add)
                        yield
                        Ycur = QtYn[:, 128:256]
                        QtYc, QtYn = QtYn, QtYc
                        P.mm(hb[:, 256:384], Et2[r][:, :], Ycur)
                        yield
                        P.copy("act", Zs[r][:, 128:256], hb[:, 256:384])
                        yield
                        P.mm(hb[:, 128:256], Xn[r][:, :], Zs[r][:, 128:256])
                        yield
                        P.tt("dve", QtYn[:, 128:256], hb[:, 128:256], Ycur, ALU.add)
                        Y = QtYn[:, 128:256]
                    else:
                        P.copy("pool", QtYc[0:C, 0:C], Mp[r][0:C, 0:C])
                        yield
                        yield from neumann(Np[r], QtYc, QtYn, 2)
                        QtYc, QtYn = res_[0]
                        Y = QtYc[0:C, 128:128 + C]
                    yield
                    if nseq == 1:
                        Sf = [Sf_p[h][:, :]]; Sb = [Sb_p[h][:, :]]
                        kgs = [kgT[r][:, 0:C]]; qds = [qdT[r][:, 0:C]]; kds = [kdec[r][0:C, :]]
                    else:
                        P.dma("sp", Sf_s[:], View(sdn, sdn.h[:, h, :, :].rearrange("s k v -> k s v")))
                        P.copy("act", Sb_s[:], Sf_s[:])
                        P.tt("dve", kgT_m[:], kgT[r][:, 0:C].un(1).bc([128, NS, C]), BMf[:], ALU.mult)
                        P.tt("dve", qdT_m[:], qdT[r][:, 0:C].un(1).bc([128, NS, C]), BMf[:], ALU.mult)
                        P.tt("dve", kdec_m[:], kdec[r][0:C, :].un(1).bc([C, NS, 128]), BMp[:], ALU.mult)
                        Sf = [Sf_s[:, s, :] for s in range(NS)]; Sb = [Sb_s[:, s, :] for s in range(NS)]
                        kgs = [kgT_m[:, s, :] for s in range(NS)]; qds = [qdT_m[:, s, :] for s in range(NS)]
                        kds = [kdec_m[:, s, :] for s in range(NS)]
                    for s in range(nseq):
                        P.mm(hb[0:C, 0:128], kgs[s], Sb[s], start=(s == 0), stop=(s == nseq - 1))
                    yield
                    P.tt("dve", Rp[r][0:C, :], vsb[r][0:C, :], hb[0:C, 0:128], ALU.subtract)
                    yield
                    P.mm(hb[0:C, 128:256], Y, Rp[r][0:C, :])
                    yield
                    P.act(u_[r][0:C, :], hb[0:C, 128:256], AF.Identity, scale=beta[0:C, h:h + 1])
                    yield
                    for s in range(nseq):
                        P.mm(ob[:, h * C:(h + 1) * C], Sb[s], qds[s], start=(s == 0), stop=False)
                    P.mm(ob[:, h * C:(h + 1) * C], u_[r][0:C, :], attnT[r][0:C, 0:C], start=False, stop=True)
                    for s0 in range(0, nseq, 4):
                        for s in range(s0, min(s0 + 4, nseq)):
                            P.mm(hb[:, (s - s0) * 128:(s - s0 + 1) * 128], kds[s], u_[r][0:C, :])
                        yield
                        for s in range(s0, min(s0 + 4, nseq)):
                            lastc = (s + 1) * T - 1 if nseq > 1 else C - 1
                            P.stt(Sf[s], Sf[s], eGrow[:, h, lastc:lastc + 1], hb[:, (s - s0) * 128:(s - s0 + 1) * 128],
                                  ALU.mult, ALU.add)
                            if nseq == 1:
                                P.copy("pool", Sb[s], Sf[s])
                        yield
                    if nseq > 1:
                        P.dma("sp", View(dn_s, dn_s.h[:, h, :, :].rearrange("s k v -> k s v")), Sf_s[:], is_output=True)

                gens = [dn_head(h) for h in range(4)]
                if nseq > 1:
                    for gen in gens:
                        for _ in gen:
                            pass
                    if nxt is not None:
                        for _ in nxt:
                            pass
                else:
                    if nxt is not None:
                        gens.append(nxt)
                    while gens:
                        for gen in list(gens):
                            try:
                                next(gen)
                            except StopIteration:
                                gens.remove(gen)
                head_norm(g, ob, zs[:, :, tsl], wdnn, omix_dn[:, :, gtok0:gtok0 + C], hn)
                if nseq == 1 and st == g.nst - 1:
                    for i in range(4):
                        P.dma("sp", View(dn_p, dn_p.h[i, :, :]), Sf_p[i][:], is_output=True)

            f0 = front(stages[0][0], stages[0][1], 0)
            for _ in f0:
                pass
            for si, (g_, st_) in enumerate(stages):
                nxt = front(stages[si + 1][0], stages[si + 1][1], (si + 1) % 2) if si + 1 < len(stages) else None
                heads(g_, st_, si % 2, nxt)
            P.emit_phase()
        P.es = ges

        with contextlib.ExitStack() as pes:
            P.es = pes
            rr["mm"][1] = [0, 1]
            NB = NCOL - C_GQ
            wB = P.sb("w_inB", [128, 8, NB], BF16)
            for k2 in range(2):
                P.dma("pool", wB[:, 4 * k2:4 * k2 + 4, :],
                      View(w_in, w_in.h.rearrange("(k p) c -> p k c", p=128)[:, 4 * k2:4 * k2 + 4, C_GQ:NCOL]))
            wo = P.sb("w_o_sb", [128, 8, D], BF16)
            P.dma("pool", wo[:], View(w_o, w_o.h.rearrange("(k p) c -> p k c", p=128)))
            oGQ, oGK, oGV, oGR, oGG = 0, 512, 1024, 1536, 2048
            ntm = norm_tmps("B")
            xt2 = [P.sb("B_xt%d" % i, [128, 1, D], F32) for i in range(2)]
            hT = [P.sb("B_hT%d" % i, [128, TS_P], BF16) for i in range(8)]
            gq2 = [P.sb("B_gq%d" % i, [128, 4, TS_P], F32) for i in range(2)]
            gk2 = [P.sb("B_gk%d" % i, [128, 4, TS_P], F32) for i in range(2)]
            grs2 = [P.sb("B_grs%d" % i, [128, 4, TS_P], BF16) for i in range(2)]
            ggT = P.sb("B_ggT", [16, TS_P], BF16)
            gk_tok = P.sb("B_gktok", [128, 512], F32)
            gv_tok2 = [P.sb("B_gvtok%d" % i, [128, 512], BF16) for i in range(2)]
            glog2 = [P.sb("B_glog%d" % i, [128, 512], F32) for i in range(2)]
            eBL = P.sb("B_eBL", [128, 512], F32)
            kdec_g2 = [P.sb("B_kdecg%d" % i, [128, 512], BF16) for i in range(2)]
            eP = [P.sb("B_eP%d" % i, [128, 128], F32) for i in range(4)]
            eN = [P.sb("B_eN%d" % i, [128, 128], F32) for i in range(4)]
            gqd = [P.sb("B_gqd%d" % i, [128, 128], BF16) for i in range(4)]
            gki = [P.sb("B_gki%d" % i, [128, 128], BF16) for i in range(4)]
            attg = [P.sb("B_attg%d" % i, [128, 128], BF16) for i in range(4)]
            omix_g2 = [P.sb("B_omixg%d" % i, [128, 4, TS_P], BF16) for i in range(2)]
            hn = (P.sb("B_hsq", [128, 512], BF16), P.sb("B_hsd", [128, 512], F32), P.sb("B_hrs", [128, 512], F32),
                  P.sb("B_hon", [128, 512], F32))
            Sf_p = [P.sb("B_Sf_p%d" % i, [128, 128], F32) for i in range(4)]
            Sb_p = [P.sb("B_Sb_p%d" % i, [128, 128], BF16) for i in range(4)]
            for i in range(4):
                P.memset("pool", Sf_p[i][:], 0.0)
                P.memset("pool", Sb_p[i][:], 0.0)
            Sf_s = P.sb("B_Sf_s", [128, NS, 128], F32)
            Sb_s = P.sb("B_Sb_s", [128, NS, 128], BF16)
            P.memset("pool", Sf_s[:], 0.0)
            gqd_m = P.sb("B_gqdm", [128, NS, 64], BF16)
            kdg_m = P.sb("B_kdgm", [64, NS, 128], BF16)
            mixt2 = [P.sb("B_mixt%d" % i, [128, 512], F32) for i in range(2)]
            stagesB = [(g_, st_) for g_ in (GP, GS) for st_ in range(g_.nst)]

            def frontB_(g, st, par):
                C, TS, nseq, T = g.C, g.TS, g.nseq, g.T
                tsl = slice(0, C)
                xt, gq, gk, grs, glog, kdec_g, gv_tok = xt2[par], gq2[par], gk2[par], grs2[par], glog2[par], kdec_g2[par], gv_tok2[par]
                load_x(g, st, xt)
                norm_to_T(g, xt[0:C, 0, :], hT, 0, sc1T, 0, ntm)
                yield
                for cc in range(16):
                    if 8 <= cc < 12:
                        continue
                    pb = bank("mm")
                    for k in range(8):
                        P.mm(pb[:, 0:TS], wB[:, k, cc * 128:(cc + 1) * 128], hT[k][:, 0:TS], start=(k == 0), stop=(k == 7))
                    if cc < 4:
                        P.copy("act", gq[:, cc, 0:TS], pb[:, 0:TS])
                    elif cc < 8:
                        P.copy("act", gk[:, cc - 4, 0:TS], pb[:, 0:TS])
                    else:
                        P.act(grs[:, cc - 12, 0:TS], pb[:, 0:TS], AF.Silu)
                    if cc % 2 == 1:
                        yield
                pb = bank("mm")
                for k in range(8):
                    P.mm(pb[0:16, 0:TS], wB[:, k, oGG:oGG + 16], hT[k][:, 0:TS], start=(k == 0), stop=(k == 7))
                P.copy("act", ggT[:, 0:TS], pb[0:16, 0:TS])
                yield
                for (off, dst) in ((oGK, gk_tok), (oGV, gv_tok)):
                    pb = bank("mm")
                    for k in range(8):
                        P.mm(pb[0:C, :], hT[k][:, tsl], wB[:, k, off:off + 512], start=(k == 0), stop=(k == 7))
                    P.copy("act", dst[0:C, :], pb[0:C, :])
                    yield
                pb = bank("mm")
                P.mm(pb[0:C, :], ggT[0:16, tsl], wg2_b[:, :], start=True, stop=False)
                P.mm(pb[0:C, :], ones_b[0:1, 0:C], bg_b[0:1, :], start=False, stop=True)
                P.act(glog[0:C, :], pb[0:C, :], AF.Exp, scale=-1.0)
                P.act(glog[0:C, :], glog[0:C, :], AF.Ln, bias=1.0)
                P.ts("dve", glog[0:C, :], glog[0:C, :], -1.0 / 16.0, ALU.mult)
                yield
                pb = bank("mm")
                P.mm(pb[0:C, :], g.LSm[:], glog[0:C, :])
                P.act(eBL[0:C, :], pb[0:C, :], AF.Exp)
                P.tt("dve", kdec_g[0:C, :], gk_tok[0:C, :], eBL[0:C, :], ALU.mult)
                yield

            def headsB_(g, st, par, nxt, post_prev=None):
                C, TS, nseq, T = g.C, g.TS, g.nseq, g.T
                c = 0
                nch = 1
                tsl = slice(0, C)
                gtok0 = g.tok0 + st * TS
                xt, gq, gk, grs, glog, kdec_g, gv_tok = xt2[par], gq2[par], gk2[par], grs2[par], glog2[par], kdec_g2[par], gv_tok2[par]
                ob = B[6 + par]
                def gla_head(h):
                    hb = B[2 + h]
                    hs = slice(h * 128, (h + 1) * 128)
                    P.mm(hb[:, 0:C], glog[0:C, hs], g.UI[:])
                    yield
                    P.act(eP[h][:, 0:C], hb[:, 0:C], AF.Exp)
                    P.act(eN[h][:, 0:C], hb[:, 0:C], AF.Exp, scale=-1.0)
                    yield
                    P.stt(gqd[h][:, 0:C], gq[:, h, tsl], DKS, eP[h][:, 0:C], ALU.mult, ALU.mult)
                    P.tt("pool", gki[h][:, 0:C], gk[:, h, tsl], eN[h][:, 0:C], ALU.mult)
                    yield
                    P.mm(hb[0:C, 128:128 + C], gki[h][:, 0:C], gqd[h][:, 0:C])
                    yield
                    P.tt("dve", attg[h][0:C, 0:C], hb[0:C, 128:128 + C], g.UI[:], ALU.mult)
                    yield
                    if nseq == 1:
                        Sf = [Sf_p[h][:, :]]; Sb = [Sb_p[h][:, :]]
                        qds = [gqd[h][:, 0:C]]; kds = [kdec_g[0:C, hs]]
                    else:
                        P.dma("sp", Sf_s[:], View(sgla, sgla.h[:, h, :, :].rearrange("s k v -> k s v")))
                        P.copy("act", Sb_s[:], Sf_s[:])
                        P.tt("dve", gqd_m[:], gqd[h][:, 0:C].un(1).bc([128, NS, C]), BMf[:], ALU.mult)
                        P.tt("dve", kdg_m[:], kdec_g[0:C, hs].un(1).bc([C, NS, 128]), BMp[:], ALU.mult)
                        Sf = [Sf_s[:, s, :] for s in range(NS)]; Sb = [Sb_s[:, s, :] for s in range(NS)]
                        qds = [gqd_m[:, s, :] for s in range(NS)]; kds = [kdg_m[:, s, :] for s in range(NS)]
                    for s in range(nseq):
                        P.mm(ob[:, h * C:(h + 1) * C], Sb[s], qds[s], start=(s == 0), stop=False)
                    P.mm(ob[:, h * C:(h + 1) * C], gv_tok[0:C, hs], attg[h][0:C, 0:C], start=False, stop=True)
                    for s0 in range(0, nseq, 4):
                        for s in range(s0, min(s0 + 4, nseq)):
                            P.mm(hb[:, (s - s0) * 128:(s - s0 + 1) * 128], kds[s], gv_tok[0:C, hs])
                        yield
                        for s in range(s0, min(s0 + 4, nseq)):
                            lastc = (s + 1) * T - 1 if nseq > 1 else C - 1
                            P.stt(Sf[s], Sf[s], eP[h][:, lastc:lastc + 1], hb[:, (s - s0) * 128:(s - s0 + 1) * 128],
                                  ALU.mult, ALU.add)
                            if nseq == 1:
                                P.copy("pool", Sb[s], Sf[s])
                        yield
                    if nseq > 1:
                        P.dma("sp", View(gla_s, gla_s.h[:, h, :, :].rearrange("s k v -> k s v")), Sf_s[:], is_output=True)

                gens = [gla_head(h) for h in range(4)]
                extras = [x for x in (post_prev, nxt) if x is not None]
                if nseq > 1:
                    for gen in extras + gens:
                        for _ in gen:
                            pass
                else:
                    pending = nxt
                    if post_prev is not None:
                        gens = gens + [post_prev]
                    elif pending is not None:
                        gens = gens + [pending]
                        pending = None
                    while gens:
                        for gen in list(gens):
                            try:
                                next(gen)
                            except StopIteration:
                                gens.remove(gen)
                                if gen is post_prev and pending is not None:
                                    gens.append(pending)
                                    pending = None
                if nseq == 1 and st == g.nst - 1:
                    for i in range(4):
                        P.dma("sp", View(gla_p, gla_p.h[i, :, :]), Sf_p[i][:], is_output=True)

            def postB_(g, st, par):
                C, TS, nseq, T = g.C, g.TS, g.nseq, g.T
                c = 0
                nch = 1
                tsl = slice(0, C)
                gtok0 = g.tok0 + st * TS
                xt, grs = xt2[par], grs2[par]
                omix_g = omix_g2[par]
                ob = B[6 + par]
                yield from head_norm_gen(g, ob, grs[:, :, tsl], wglan, omix_g[:, :, tsl], hn, via_act=(par == 1))
                for n in range(2):
                    pb = bank("mm")
                    for k in range(8):
                        lhs = omix_dn[:, k, gtok0:gtok0 + C] if k < 4 else omix_g[:, k - 4, tsl]
                        P.mm(pb[0:C, :], lhs, wo[:, k, n * 512:(n + 1) * 512], start=(k == 0), stop=(k == 7))
                    yield
                    P.tt("dve", mixt2[n][0:C, :], pb[0:C, :], g.gtok[:, 0, n * 512:(n + 1) * 512], ALU.mult)
                    yield
                    P.tt("pool", xt[0:C, c, n * 512:(n + 1) * 512], xt[0:C, c, n * 512:(n + 1) * 512], mixt2[n][0:C, :], ALU.add)
                    yield
                r0 = st * TS
                P.dma("sp", View(g.x1d, g.x1d.h[r0:r0 + TS, :].rearrange("(c p) d -> p c d", p=C)), xt[0:C, 0:nch, :],
                      owner=xt)

            fb0 = frontB_(stagesB[0][0], stagesB[0][1], 0)
            for _ in fb0:
                pass
            post_prev = None
            for si, (g_, st_) in enumerate(stagesB):
                nxt = frontB_(stagesB[si + 1][0], stagesB[si + 1][1], (si + 1) % 2) if si + 1 < len(stagesB) else None
                headsB_(g_, st_, si % 2, nxt, post_prev)
                post_prev = postB_(g_, st_, si % 2)
            for _ in post_prev:
                pass
            P.emit_phase()
        P.es = ges

        abes.close()
        NTOK = LP + NS * LS
        NCHK = 17
        with contextlib.ExitStack() as pes:
            P.es = pes
            rr["mm"][1] = [2, 3, 6]
            h2T = [P.sb("h2T%d" % i, [128, NTOK], BF16) for i in range(8)]
            acc = [P.sb("x2acc%d" % i, [128, D], F32) for i in range(NCHK)]
            ntm = norm_tmps("M")
            xt = [P.sb("M_xt%d" % i, [128, 1, D], F32) for i in range(2)]
            chunks = [(GP, i) for i in range(16)] + [(GS, 0)]

            def prep(ci):
                g, i = chunks[ci]
                C = g.C
                x_ = xt[ci % 2]
                P.dma("sp", x_[0:C, 0, :], View(g.x1d, g.x1d.h[i * C:(i + 1) * C, :]))
                norm_to_T(g, x_[0:C, 0, :], h2T, g.tok0 + i * C, sc2T, 24, ntm)
            NE = 8
            FE = 4
            wu = [P.sb("wu%d" % i, [128, 8, 512], BF16) for i in range(2)]
            wd = [P.sb("wd%d" % i, [128, FE, D], BF16) for i in range(2)]
            actT = P.sb("actT", [128, FE, NTOK], BF16)
            sqt = [P.sb("M_sq%d" % i, [128, 512], BF16) for i in range(2)]
            ttiles = [(t0, min(512, NTOK - t0)) for t0 in range(0, NTOK, 512)]
            wnf = P.sb("wnf", [128, D], F32)
            P.dma("sp", wnf[:], bcast_rows(w_norm_f, 128, D))
            fss2 = [P.sb("F_ss%d" % i, [128, 1], F32) for i in range(2)]
            fsd2 = [P.sb("F_sd%d" % i, [128, 1], F32) for i in range(2)]
            frs2 = [P.sb("F_rs%d" % i, [128, 1], F32) for i in range(2)]
            fjunk2 = [ntm[0], P.sb("F_junk1", [128, D], BF16)]

            def final_load(ci):
                g, i = chunks[ci]
                P.dma("sp", xt[ci % 2][0:g.C, 0, :], View(g.x1d, g.x1d.h[i * g.C:(i + 1) * g.C, :]))

            def final(ci):
                g, i = chunks[ci]
                C = g.C
                x_ = xt[ci % 2]
                if ci == 0:
                    final_load(0)
                if ci + 1 < len(chunks):
                    final_load(ci + 1)
                a_ = acc[ci][0:C, :]
                fss, fsd, frs, fjunk = fss2[ci % 2], fsd2[ci % 2], frs2[ci % 2], fjunk2[ci % 2]
                P.tt("dve", a_, a_, g.gtok[:, 1, :], ALU.mult)
                P.tt("dve" if ci % 2 == 0 else "pool", a_, a_, x_[0:C, 0, :], ALU.add)
                P.act(fjunk[0:C, :], a_, AF.Square, accum=fss[0:C, :])
                P.act(fsd[0:C, :], fss[0:C, :], AF.Ln, bias=EPS, scale=1.0 / D)
                P.act(frs[0:C, :], fsd[0:C, :], AF.Exp, scale=-0.5)
                P.stt(x_[0:C, 0, :], a_, frs[0:C, :], wnf[0:C, :], ALU.mult, ALU.mult)
                P.dma("sp", View(g.y, g.y.h[i * C:(i + 1) * C, :]), x_[0:C, 0, :], is_output=True)

            ei = [0]

            def up(wub, f, t0, tw):
                pb = bank("mm")
                for k in range(8):
                    P.mm(pb[:, 0:tw], wub[:, k, f * 128:(f + 1) * 128], h2T[k][:, t0:t0 + tw], start=(k == 0), stop=(k == 7))
                sq = sqt[ei[0] % 2]
                ei[0] += 1
                P.act(sq[:, 0:tw], pb[:, 0:tw], AF.Square)
                P.stt(actT[:, f, t0:t0 + tw], pb[:, 0:tw], 0.0, sq[:, 0:tw], ALU.is_gt, ALU.mult)

            for e in range(NE):
                wub, wdb = wu[e % 2], wd[e % 2]
                P.dma("pool", wub[:], View(w_up, w_up.h.rearrange("(k p) c -> p k c", p=128)[:, :, e * 512:(e + 1) * 512]))
                P.dma("pool", wdb[:], View(w_down, w_down.h.rearrange("(f p) c -> p f c", p=128)[:, e * FE:(e + 1) * FE, :]))
                if e == 0:
                    for ti, (t0, tw) in enumerate(ttiles):
                        for ci in range(len(chunks)):
                            g, i = chunks[ci]
                            if t0 <= g.tok0 + i * g.C < t0 + tw:
                                prep(ci)
                        for f in range(FE):
                            up(wub, f, t0, tw)
                else:
                    for f in range(FE):
                        for (t0, tw) in ttiles:
                            up(wub, f, t0, tw)
                for ci, (g, i) in enumerate(chunks):
                    C = g.C
                    tk = g.tok0 + i * C
                    for n in range(2):
                        pb = bank("tr") if (ci * 2 + n) % 2 == 0 else bank("rc")
                        for f in range(FE):
                            P.mm(pb[0:C, :], actT[:, f, tk:tk + C], wdb[:, f, n * 512:(n + 1) * 512], start=(f == 0), stop=(f == FE - 1))
                        dst = acc[ci][0:C, n * 512:(n + 1) * 512]
                        if e == 0:
                            P.copy("act", dst, pb[0:C, :])
                        else:
                            P.tt("dve", dst, dst, pb[0:C, :], ALU.add)
                    if e == NE - 1:
                        final(ci)
            P.emit_phase(final=True)
        P.es = ges
        print("instructions:", P.ninstr)
    return nc


_NC = None


def kernel(x_prompt, x_sample, state_dn_conv, state_dn, state_gla, c_prompt, c_sample, w_ada, b_ada,
           w_norm1, w_in, w_conv, dn_a_log, dn_dt_bias, w_dn_norm, w_gla_g2, b_gla_g, w_gla_norm, w_o,
           w_norm2, w_up, w_down, w_norm_f):
    global _NC
    if _NC is None:
        _NC = build()
    nc = _NC
    f = lambda a: np.ascontiguousarray(np.asarray(a, dtype=np.float32))
    shared = {
        "w_ada": f(w_ada[0]), "b_ada": f(b_ada[0]), "w_norm1": f(w_norm1[0]), "w_in": f(w_in[0]),
        "w_conv": f(w_conv[0]), "dn_a_log": f(dn_a_log[0]), "dn_dt_bias": f(dn_dt_bias[0]),
        "w_dn_norm": f(w_dn_norm[0]), "w_gla_g2": f(w_gla_g2[0]), "b_gla_g": f(b_gla_g[0]),
        "w_gla_norm": f(w_gla_norm[0]), "w_o": f(w_o[0]), "w_norm2": f(w_norm2[0]), "w_up": f(w_up[0]),
        "w_down": f(w_down[0]), "w_norm_f": f(w_norm_f),
    }
    in_maps = []
    for c in range(8):
        sl = slice(NS * c, NS * (c + 1))
        m = dict(shared)
        m["xp"] = f(x_prompt[c])
        m["xs"] = f(np.asarray(x_sample)[sl].reshape(NS * LS, D))
        m["convs"] = f(np.asarray(state_dn_conv)[0, sl].reshape(NS * 3, 1536))
        m["sdn"] = f(np.asarray(state_dn)[0, sl])
        m["sgla"] = f(np.asarray(state_gla)[0, sl])
        m["cvec"] = f(np.concatenate([np.asarray(c_prompt)[c:c + 1], np.asarray(c_sample)[sl]], axis=0))
        in_maps.append(m)
    res = run_bass_kernel_spmd(nc, in_maps, core_ids=list(range(8)))
    R = res.results
    y_prompt = np.stack([R[c]["y_p"] for c in range(8)], axis=0)
    y_sample = np.concatenate([R[c]["y_s"].reshape(NS, LS, D) for c in range(8)], axis=0)
    conv_pp = np.stack([R[c]["conv_p"] for c in range(8)], axis=0)[None]
    dn_pp = np.stack([R[c]["dn_p"] for c in range(8)], axis=0)[None]
    gla_pp = np.stack([R[c]["gla_p"] for c in range(8)], axis=0)[None]
    conv_ss = np.concatenate([R[c]["conv_s"].reshape(NS, 3, 1536) for c in range(8)], axis=0)[None]
    dn_ss = np.concatenate([R[c]["dn_s"] for c in range(8)], axis=0)[None]
    gla_ss = np.concatenate([R[c]["gla_s"] for c in range(8)], axis=0)[None]
    outs = (y_prompt, y_sample, conv_pp, dn_pp, gla_pp, conv_ss, dn_ss, gla_ss)
    return tuple(np.ascontiguousarray(o, dtype=np.float32) for o in outs)
```

```python
import contextlib
import numpy as np
import concourse.bass as bass
import concourse.mybir as mybir
from concourse.bass_utils import run_bass_kernel_spmd

F32 = mybir.dt.float32
BF16 = mybir.dt.bfloat16
AF = mybir.ActivationFunctionType
ALU = mybir.AluOpType

ENGS = ("pe", "act", "dve", "pool", "sp")
N_DSEM = 72


class View:
    __slots__ = ("t", "ap")

    def __init__(self, t, ap):
        self.t = t
        self.ap = ap

    def __getitem__(self, idx):
        return View(self.t, self.ap[idx])

    def bc(self, shape):
        return View(self.t, self.ap.to_broadcast(list(shape)))

    def re(self, s, **kw):
        return View(self.t, self.ap.rearrange(s, **kw))

    def un(self, axis):
        return View(self.t, self.ap.unsqueeze(axis))


class Tile:
    def __init__(self, name, h):
        self.name = name
        self.h = h
        self.w = None
        self.rs = []
        self.dsem = None
        self.dcount = 0

    def __getitem__(self, idx):
        return View(self, self.h[idx])

    def bitcast(self, dt):
        return View(self, self.h.bitcast(dt)[:])


class Prog:
    def __init__(self, nc, es):
        self.nc = nc
        self.ges = es
        self.es = es
        self.q = {e: [] for e in ENGS}
        self.sem = {e: es.enter_context(nc.semaphore("cs_" + e)) for e in ENGS}
        self.cnt = {e: 0 for e in ENGS}
        self.waited = {e: {} for e in ENGS}
        self.out_events = []
        self.dpool = [[es.enter_context(nc.semaphore("ds%d" % i)), 0] for i in range(N_DSEM)]
        self.dfree = {"sw": list(range(0, 24)), "hw": list(range(24, N_DSEM))}
        self.dtiles = []
        self.ninstr = 0
        self.muted = False
        self.nphase = 0
        self.total = 0
        import os
        self.kops = int(os.environ.get("KOPS", "100000000"))
        self.kmark = bool(os.environ.get("KMARK"))

    def sb(self, name, shape, dt, persist=False):
        es = self.ges if persist else self.es
        return Tile(name, es.enter_context(self.nc.sbuf_tensor(name, list(shape), dt)))

    def ps(self, name, shape, dt=F32):
        return Tile(name, self.ges.enter_context(self.nc.psum_tensor(name, list(shape), dt)))

    def dram(self, name, shape, dt, kind=None):
        if kind is None:
            h = self.nc.dram_tensor(name, list(shape), dt)
        else:
            h = self.nc.dram_tensor(name, list(shape), dt, kind=kind)
        return Tile(name, h)

    def _deps(self, eng, reads, writes):
        evs = []
        for t in reads:
            if t.w is not None:
                evs.append(t.w)
        for t in writes:
            if t.w is not None:
                evs.append(t.w)
            for r in t.rs:
                evs.append(r)
        waits = []
        w = self.waited[eng]
        for (peng, sem, val) in evs:
            if peng == "pe" and eng == "pe":
                continue
            if w.get(sem.num, 0) >= val:
                continue
            w[sem.num] = val
            waits.append((sem, val))
        return waits

    def _commit(self, ev, reads, writes):
        for t in reads:
            t.rs.append(ev)
        for t in writes:
            t.w = ev
            t.rs = []

    def mark(self, name):
        if self.kmark:
            print("MARK", name, self.total)

    def op(self, eng, fn, reads=(), writes=()):
        if self.muted:
            return None
        self.total += 1
        if self.total > self.kops:
            return None
        reads = list(dict.fromkeys(reads))
        writes = list(dict.fromkeys(writes))
        waits = self._deps(eng, reads, writes)
        self.cnt[eng] += 1
        ev = (eng, self.sem[eng], self.cnt[eng])
        self.q[eng].append((waits, fn, (self.sem[eng], 1)))
        self._commit(ev, [t for t in reads if t not in writes], writes)
        return ev

    def dma(self, queue, out, in_, owner=None, is_output=False, **kw):
        if self.muted:
            return None
        self.total += 1
        if self.total > self.kops:
            return None
        reads = [in_.t]
        writes = [out.t]
        if owner is None:
            owner = in_.t if is_output else out.t
        qc = "sw" if queue == "pool" else "hw"
        if owner.dsem is None:
            owner.dsem = {}
        if qc not in owner.dsem:
            owner.dsem[qc] = self.dfree[qc].pop()
            self.dtiles.append((owner, qc))
        pe = self.dpool[owner.dsem[qc]]
        waits = self._deps(queue, reads, writes)
        pe[1] += 16
        ev = ("dma", pe[0], pe[1])
        oap, iap = out.ap, in_.ap
        sem = pe[0]
        self.q[queue].append((waits, lambda e: e.dma_start(out=oap, in_=iap, **kw), (sem, 16)))
        self._commit(ev, reads, writes)
        if is_output:
            self.out_events.append(ev)
        return ev

    def dump(self, name, view, shape, dt=F32):
        import os
        if not os.environ.get("KDEBUG"):
            return
        t = self.dram("dbg_" + name, shape, dt, kind="ExternalOutput")
        self.dma("sp", View(t, t.h[:]), view, is_output=True)

    def mm(self, out, lhsT, rhs, start=True, stop=True):
        o, l, r = out.ap, lhsT.ap, rhs.ap
        return self.op("pe", lambda e: e.matmul(o, lhsT=l, rhs=r, start=start, stop=stop),
                       [lhsT.t, rhs.t], [out.t])

    def tr(self, out, in_, ident):
        o, i, d = out.ap, in_.ap, ident.ap
        return self.op("pe", lambda e: e.transpose(o, i, d), [in_.t, ident.t], [out.t])

    def act(self, out, in_, func, bias=None, scale=None, accum=None):
        o, i = out.ap, in_.ap
        kw = {}
        reads = [in_.t]
        writes = [out.t]
        if bias is not None:
            if isinstance(bias, View):
                kw["bias"] = bias.ap
                reads.append(bias.t)
            else:
                kw["bias"] = bias
        if scale is not None:
            if isinstance(scale, View):
                kw["scale"] = scale.ap
                reads.append(scale.t)
            else:
                kw["scale"] = scale
        if accum is not None:
            kw["accum_out"] = accum.ap
            writes.append(accum.t)
        return self.op("act", lambda e: e.activation(out=o, in_=i, func=func, **kw), reads, writes)

    def tt(self, eng, out, a, b, op):
        o, x, y = out.ap, a.ap, b.ap
        return self.op(eng, lambda e: e.tensor_tensor(out=o, in0=x, in1=y, op=op), [a.t, b.t], [out.t])

    def ts(self, eng, out, in_, s1, op0, s2=None, op1=None):
        o, i = out.ap, in_.ap
        reads = [in_.t]
        a1, a2 = s1, s2
        if isinstance(s1, View):
            a1 = s1.ap
            reads.append(s1.t)
        if isinstance(s2, View):
            a2 = s2.ap
            reads.append(s2.t)
        if op1 is None:
            return self.op(eng, lambda e: e.tensor_scalar(out=o, in0=i, scalar1=a1, scalar2=None, op0=op0),
                           reads, [out.t])
        return self.op(eng, lambda e: e.tensor_scalar(out=o, in0=i, scalar1=a1, scalar2=a2, op0=op0, op1=op1),
                       reads, [out.t])

    def stt(self, out, in0, scalar, in1, op0, op1):
        o, x, y = out.ap, in0.ap, in1.ap
        reads = [in0.t, in1.t]
        sc = scalar
        if isinstance(scalar, View):
            sc = scalar.ap
            reads.append(scalar.t)
        return self.op("dve", lambda e: e.scalar_tensor_tensor(out=o, in0=x, scalar=sc, in1=y, op0=op0, op1=op1),
                       reads, [out.t])

    def copy(self, eng, out, in_):
        o, i = out.ap, in_.ap
        if eng == "act":
            return self.op("act", lambda e: e.activation(out=o, in_=i, func=AF.Copy), [in_.t], [out.t])
        return self.op(eng, lambda e: e.tensor_copy(out=o, in_=i), [in_.t], [out.t])

    def recip(self, out, in_):
        o, i = out.ap, in_.ap
        return self.op("dve", lambda e: e.reciprocal(out=o, in_=i), [in_.t], [out.t])

    def memset(self, eng, out, val):
        o = out.ap
        return self.op(eng, lambda e: e.memset(o, val), [], [out.t])

    def aselect(self, out, in_, pattern, cmp, fill, base, cm):
        o, i = out.ap, in_.ap
        return self.op("pool", lambda e: e.affine_select(out=o, in_=i, pattern=pattern, compare_op=cmp, fill=fill,
                                                         base=base, channel_multiplier=cm), [in_.t], [out.t])

    def emit_phase(self, final=False):
        import os
        if self.muted:
            return
        self.nphase += 1
        if self.nphase > int(os.environ.get("KSTOP", "99")):
            self.muted = True
        nc = self.nc
        q = self.q
        self.ninstr += sum(len(v) for v in q.values())
        if not hasattr(self, "newcnt"):
            self.newcnt = {e: 0 for e in ENGS}
            self.oldmap = {e: {0: 0} for e in ENGS}
        num2eng = {self.sem[e].num: e for e in ENGS}
        sig = {e: set() for e in ENGS}
        for name in ENGS:
            for (waits, fn, inc) in q[name]:
                for (sem, val) in waits:
                    e2 = num2eng.get(sem.num)
                    if e2 is not None:
                        sig[e2].add(val)
        for e in ENGS:
            if self.cnt[e] > 0:
                sig[e].add(self.cnt[e])
        for e in ENGS:
            for v in sorted(sig[e]):
                if v not in self.oldmap[e]:
                    self.newcnt[e] += 1
                    self.oldmap[e][v] = self.newcnt[e]
        omap = self.oldmap

        def mapw(sem, val):
            e2 = num2eng.get(sem.num)
            if e2 is None:
                return val
            return omap[e2][val]

        targets = [(self.sem[e], self.cnt[e]) for e in ENGS if self.cnt[e] > 0]
        targets += [(p[0], p[1]) for p in self.dpool if p[1] > 0]
        waited = self.waited
        base = {e: self.cnt[e] - sum(1 for it in q[e] if it[2][1] == 1) for e in ENGS}

        def run(name, e):
            c = base[name]
            for (waits, fn, inc) in q[name]:
                for (sem, val) in waits:
                    e.wait_ge(sem, mapw(sem, val))
                ins = fn(e)
                if inc[1] == 16:
                    ins.then_inc(inc[0], 16)
                else:
                    c += 1
                    if c in sig[name]:
                        ins.then_inc(inc[0], 1)
            for (sem, val) in targets:
                if waited[name].get(sem.num, 0) >= val:
                    continue
                waited[name][sem.num] = val
                e.wait_ge(sem, mapw(sem, val))

        with nc.Block() as block:
            @block.sync
            def _(e):
                run("sp", e)

            @block.tensor
            def _(e):
                run("pe", e)

            @block.scalar
            def _(e):
                run("act", e)

            @block.vector
            def _(e):
                run("dve", e)

            @block.gpsimd
            def _(e):
                run("pool", e)

        for k in q:
            q[k] = []
        for (t, qc) in self.dtiles:
            self.dfree[qc].append(t.dsem.pop(qc))
        self.dtiles = []


D = 1024
KD = 8
DFF = 4096
NCOL = 4120
EPS = 1e-6
LP = 2048
NS = 16
LS = 4
TS_P = 128
C_Q0, C_K0, C_V0, C_Z0, C_A0, C_B0 = 0, 512, 1024, 1536, 2048, 2052
C_GQ, C_GK, C_GV, C_GR, C_GG = 2056, 2568, 3080, 3592, 4104
DKS = 128.0 ** -0.5


class Grp:
    pass


def build(debug=False):
    nc = bass.Bass("TRN2", target_bir_lowering=False)
    ges = contextlib.ExitStack()
    with ges:
        P = Prog(nc, ges)
        I = lambda name, shape: P.dram(name, shape, F32, kind="ExternalInput")
        O = lambda name, shape: P.dram(name, shape, F32, kind="ExternalOutput")
        xp = I("xp", [LP, D]); xs = I("xs", [NS * LS, D])
        convs = I("convs", [NS * 3, 1536]); sdn = I("sdn", [NS, 4, 128, 128]); sgla = I("sgla", [NS, 4, 128, 128])
        cvec = I("cvec", [17, D])
        w_ada = I("w_ada", [D, 6 * D]); b_ada = I("b_ada", [6 * D]); w_norm1 = I("w_norm1", [D])
        w_in = I("w_in", [D, NCOL]); w_conv = I("w_conv", [4, 1536]); a_log = I("dn_a_log", [4])
        dt_bias = I("dn_dt_bias", [4]); w_dn_norm = I("w_dn_norm", [128]); w_g2 = I("w_gla_g2", [16, 512])
        b_g = I("b_gla_g", [512]); w_gla_norm = I("w_gla_norm", [128]); w_o = I("w_o", [D, D])
        w_norm2 = I("w_norm2", [D]); w_up = I("w_up", [D, DFF]); w_down = I("w_down", [DFF, D])
        w_norm_f = I("w_norm_f", [D])
        y_p = O("y_p", [LP, D]); y_s = O("y_s", [NS * LS, D])
        conv_p = O("conv_p", [3, 1536]); dn_p = O("dn_p", [4, 128, 128]); gla_p = O("gla_p", [4, 128, 128])
        conv_s = O("conv_s", [NS * 3, 1536]); dn_s = O("dn_s", [NS, 4, 128, 128]); gla_s = O("gla_s", [NS, 4, 128, 128])
        x1d_p = P.dram("x1d_p", [LP, D], F32)
        x1d_s = P.dram("x1d_s", [NS * LS, D], F32)
        dbg = {}

        def bcast_rows(t, n, width, off=0):
            return View(t, bass.AP(t.h, off, [[0, n], [1, width]]))

        B = [P.ps("bank%d" % i, [128, 512]) for i in range(8)]
        rr = {"tr": [0, [0, 1]], "mm": [0, [0, 1]], "rc": [0, [4, 5]], "mmB": [0, [2, 3]]}

        def bank(pool):
            st = rr[pool]
            b = B[st[1][st[0] % len(st[1])]]
            st[0] += 1
            return b

        ident_f = P.sb("ident_f", [128, 128], F32, persist=True)
        ones_f = P.sb("ones_f", [128, 128], F32, persist=True)
        ones_b = P.sb("ones_b", [128, 128], BF16, persist=True)
        P.memset("pool", ones_f[:], 1.0)
        P.memset("pool", ones_b[:], 1.0)
        P.aselect(ident_f[:], ones_f[:], [[-1, 128]], ALU.is_equal, 0.0, 0, 1)
        ident_b = P.sb("ident_b", [128, 128], BF16, persist=True)
        P.copy("pool", ident_b[:], ident_f[:])

        def make_masks(g, C, bs):
            g.UI = P.sb(g.name + "UI", [C, C], F32, persist=True)
            g.US = P.sb(g.name + "US", [C, C], F32, persist=True)
            g.LSm = P.sb(g.name + "LS", [C, C], F32, persist=True)
            P.aselect(g.UI[:], ones_f[0:C, 0:C], [[1, C]], ALU.is_ge, 0.0, 0, -1)
            P.aselect(g.US[:], ones_f[0:C, 0:C], [[1, C]], ALU.is_gt, 0.0, 0, -1)
            P.aselect(g.LSm[:], ones_f[0:C, 0:C], [[-1, C]], ALU.is_gt, 0.0, 0, 1)
            if bs < C:
                nb = C // bs
                for m in (g.UI, g.US, g.LSm):
                    v = m[:].re("p (s t) -> p s t", t=bs)
                    P.aselect(v, v, [[-bs, nb], [0, bs]], ALU.is_ge, 0.0, 0, 1)
                    P.aselect(v, v, [[bs, nb], [0, bs]], ALU.is_ge, 0.0, bs - 1, -1)

        GP = Grp(); GP.name = "p"; GP.C = 128; GP.TS = TS_P; GP.nseq = 1; GP.T = TS_P; GP.nst = LP // TS_P
        GS = Grp(); GS.name = "s"; GS.C = 64; GS.TS = 64; GS.nseq = NS; GS.T = LS; GS.nst = 1
        GP.x = xp; GS.x = xs; GP.x1d = x1d_p; GS.x1d = x1d_s; GP.y = y_p; GS.y = y_s
        GP.v0 = 0; GS.v0 = 1
        GP.tok0 = 0; GS.tok0 = LP
        make_masks(GP, 128, 128)
        make_masks(GS, 64, 4)
        NC_SLOW = dict(allow_slow_non_contiguous=True)
        b_adaT = P.sb("b_adaT", [128, 48], F32, persist=True)
        P.dma("sp", b_adaT[:], View(b_ada, b_ada.h.rearrange("(c p) -> p c", p=128)), **NC_SLOW)
        wn1T = P.sb("wn1T", [128, 8], F32, persist=True)
        wn2T = P.sb("wn2T", [128, 8], F32, persist=True)
        P.dma("sp", wn1T[:], View(w_norm1, w_norm1.h.rearrange("(c p) -> p c", p=128)), **NC_SLOW)
        P.dma("sp", wn2T[:], View(w_norm2, w_norm2.h.rearrange("(c p) -> p c", p=128)), **NC_SLOW)
        wconvT = P.sb("wconvT", [128, 4, 12], F32, persist=True)
        for i in range(4):
            P.dma("sp", wconvT[:, i, :], View(w_conv, w_conv.h[i, :].rearrange("(c p) -> p c", p=128)), **NC_SLOW)
        wdnn = P.sb("wdnn", [128, 1], F32, persist=True)
        wglan = P.sb("wglan", [128, 1], F32, persist=True)
        P.dma("sp", wdnn[:], View(w_dn_norm, w_dn_norm.h.rearrange("(p o) -> p o", o=1)))
        P.dma("sp", wglan[:], View(w_gla_norm, w_gla_norm.h.rearrange("(p o) -> p o", o=1)))
        alog_bc = P.sb("alog_bc", [128, 4], F32, persist=True)
        dtb_bc = P.sb("dtb_bc", [128, 4], F32, persist=True)
        P.dma("sp", alog_bc[:], bcast_rows(a_log, 128, 4))
        P.dma("sp", dtb_bc[:], bcast_rows(dt_bias, 128, 4))
        nexpa = P.sb("nexpa", [128, 4], F32, persist=True)
        P.act(nexpa[:], alog_bc[:], AF.Exp)
        P.ts("dve", nexpa[:], nexpa[:], -1.0, ALU.mult)
        wg2_b = P.sb("wg2_b", [16, 512], BF16, persist=True)
        bg_b = P.sb("bg_b", [1, 512], BF16, persist=True)
        P.dma("pool", wg2_b[:], w_g2[:])
        P.dma("pool", bg_b[:], View(b_g, b_g.h.rearrange("(o c) -> o c", o=1)))
        bgt_b = P.sb("bgt_b", [1, 2, 1024], F32, persist=True)
        P.dma("sp", bgt_b[:, 0, :], View(b_ada, b_ada.h.rearrange("(o c) -> o c", o=1)[:, 2048:3072]))
        P.dma("sp", bgt_b[:, 1, :], View(b_ada, b_ada.h.rearrange("(o c) -> o c", o=1)[:, 5120:6144]))

        modT = P.sb("modT", [128, 48, 17], F32, persist=True)
        sc1T = P.sb("sc1T", [128, 8, 17], F32, persist=True)
        sc2T = P.sb("sc2T", [128, 8, 17], F32, persist=True)
        GP.gtok = P.sb("gtok_p", [128, 2, 1024], F32, persist=True)
        GS.gtok = P.sb("gtok_s", [64, 2, 1024], F32, persist=True)

        with contextlib.ExitStack() as pes:
            P.es = pes
            c_sb = P.sb("c_sb", [17, D], F32)
            P.dma("sp", c_sb[:], cvec[:])
            P.act(c_sb[:], c_sb[:], AF.Silu)
            cT = P.sb("cT", [128, 8, 17], F32)
            for k in range(8):
                pb = bank("tr")
                P.tr(pb[:, 0:17], c_sb[0:17, k * 128:(k + 1) * 128], ident_f[0:17, 0:17])
                P.copy("dve", cT[:, k, :], pb[:, 0:17])
            cTe_p = P.sb("cTe_p", [128, 8, 128], F32)
            cTe_s = P.sb("cTe_s", [128, 8, 64], F32)
            P.copy("dve", cTe_p[:], cT[:, :, 0:1].bc([128, 8, 128]))
            for k in range(8):
                P.copy("dve", cTe_s[:, k, :].re("p (s t) -> p s t", t=4), cT[:, k, 1:17].un(2).bc([128, 16, 4]))
            wa = [P.sb("wa%d" % i, [128, 8, 512], F32) for i in range(2)]
            wab = [P.sb("wab%d" % i, [128, 8, 512], BF16) for i in range(3)]
            cT_b = P.sb("cT_b", [128, 8, 17], BF16)
            cTe_pb = P.sb("cTe_pb", [128, 8, 128], BF16)
            cTe_sb = P.sb("cTe_sb", [128, 8, 64], BF16)
            bgt_bb = P.sb("bgt_bb", [1, 2, 1024], BF16)
            P.copy("dve", cT_b[:], cT[:])
            P.copy("dve", cTe_pb[:], cTe_p[:])
            P.copy("dve", cTe_sb[:], cTe_s[:])
            P.copy("dve", bgt_bb[:], bgt_b[:])
            for gi in range(12):
                hi_prec = gi < 4
                if hi_prec:
                    wb = wa[gi % 2]
                    P.dma("sp", wb[:], View(w_ada, w_ada.h.rearrange("(k p) c -> p k c", p=128)[:, :, gi * 512:(gi + 1) * 512]))
                    cTx = cT
                else:
                    wb = wab[gi % 3]
                    P.dma("pool", wb[:], View(w_ada, w_ada.h.rearrange("(k p) c -> p k c", p=128)[:, :, gi * 512:(gi + 1) * 512]))
                    cTx = cT_b
                for j in range(4):
                    ch = gi * 4 + j
                    pb = bank("mm")
                    for k in range(8):
                        P.mm(pb[:, 0:17], wb[:, k, j * 128:(j + 1) * 128], cTx[:, k, :], start=(k == 0), stop=(k == 7))
                    P.act(modT[:, ch, :], pb[:, 0:17], AF.Identity, bias=b_adaT[:, ch:ch + 1])
                if gi in (4, 5, 10, 11):
                    which = 0 if gi < 6 else 1
                    half = gi % 2
                    for (g, cte) in ((GP, cTe_pb), (GS, cTe_sb)):
                        pb = bank("mm")
                        for k in range(8):
                            P.mm(pb[0:g.C, :], cte[:, k, :], wb[:, k, :], start=(k == 0), stop=False)
                        P.mm(pb[0:g.C, :], ones_b[0:1, 0:g.C], bgt_bb[0:1, which, half * 512:(half + 1) * 512],
                             start=False, stop=True)
                        P.copy("act", g.gtok[:, which, half * 512:(half + 1) * 512], pb[0:g.C, :])
            for k in range(8):
                P.stt(sc1T[:, k, :], modT[:, 8 + k, :], 1.0, wn1T[:, k:k + 1].bc([128, 17]), ALU.add, ALU.mult)
                P.stt(sc2T[:, k, :], modT[:, 32 + k, :], 1.0, wn2T[:, k:k + 1].bc([128, 17]), ALU.add, ALU.mult)
            P.dump("modT", modT[:], [128, 48, 17])
            P.dump("gtok_p", GP.gtok[:], [128, 2, 1024])
            P.dump("gtok_s", GS.gtok[:], [64, 2, 1024])
            P.dump("sc1T", sc1T[:], [128, 8, 17])
            P.emit_phase()
        P.es = ges

        def load_x(g, st, xt, src=None):
            src = g.x if src is None else src
            r0 = st * g.TS
            nch = g.TS // g.C
            P.dma("sp", xt[0:g.C, 0:nch, :], View(src, src.h[r0:r0 + g.TS, :].rearrange("(c p) d -> p c d", p=g.C)))

        def norm_to_T(g, xin, hT, col0, scT, shbase, tmp, hT32=None):
            C = g.C
            junk, ss, sd, rstd, xsn = tmp[0:5]
            P.act(junk[0:C, :], xin, AF.Square, accum=ss[0:C, :])
            P.act(sd[0:C, :], ss[0:C, :], AF.Ln, bias=EPS, scale=1.0 / D)
            P.act(rstd[0:C, :], sd[0:C, :], AF.Exp, scale=-0.5)
            P.act(xsn[0:C, :], xin, AF.Identity, scale=rstd[0:C, :])
            for half in range(2):
                pb = bank("tr")
                for kk in range(4):
                    k = half * 4 + kk
                    P.tr(pb[:, kk * C:(kk + 1) * C], xsn[0:C, k * 128:(k + 1) * 128], ident_f[0:C, 0:C])
                for kk in range(4):
                    k = half * 4 + kk
                    src = pb[:, kk * C:(kk + 1) * C]
                    dstb = hT[k][:, col0:col0 + C]
                    dst = dstb if hT32 is None else hT32[k][:, col0:col0 + C]
                    if g.nseq == 1:
                        P.act(dst, src, AF.Identity, bias=modT[:, shbase + k, 0:1], scale=scT[:, k, 0:1])
                    else:
                        t2 = tmp[5]
                        P.tt("dve", t2[:, 0:C].re("p (s t) -> p s t", t=LS), src.re("p (s t) -> p s t", t=LS),
                             scT[:, k, 1:17].un(2).bc([128, NS, LS]), ALU.mult)
                        P.tt("dve", dst.re("p (s t) -> p s t", t=LS), t2[:, 0:C].re("p (s t) -> p s t", t=LS),
                             modT[:, shbase + k, 1:17].un(2).bc([128, NS, LS]), ALU.add)
                    if hT32 is not None:
                        P.copy("pool", dstb, dst)

        def norm_tmps(pfx):
            return (P.sb(pfx + "junk", [128, D], BF16), P.sb(pfx + "ss", [128, 1], F32), P.sb(pfx + "sd", [128, 1], F32),
                    P.sb(pfx + "rstd", [128, 1], F32), P.sb(pfx + "xsn", [128, D], F32), P.sb(pfx + "t2", [128, 128], F32))

        def head_norm_gen(g, ob, gate_v, wn, out_v, tmps, via_act=False):
            C = g.C
            W = 4 * C
            sq, sd, rs, on = tmps
            P.act(sq[:, 0:W], ob[:, 0:W], AF.Square)
            if via_act:
                P.copy("act", on[:, 0:W], ob[:, 0:W])
            yield
            pb = bank("mm")
            P.mm(pb[:, 0:W], ones_b[:, :], sq[:, 0:W])
            yield
            P.act(sd[:, 0:W], pb[:, 0:W], AF.Ln, bias=EPS, scale=1.0 / 128)
            P.act(rs[:, 0:W], sd[:, 0:W], AF.Exp, scale=-0.5)
            yield
            if via_act:
                P.tt("dve", on[:, 0:W], on[:, 0:W], rs[:, 0:W], ALU.mult)
            else:
                P.tt("dve", on[:, 0:W], ob[:, 0:W], rs[:, 0:W], ALU.mult)
            P.stt(out_v, on[:, 0:W].re("p (h c) -> p h c", c=C), wn[:, 0:1], gate_v, ALU.mult, ALU.mult)
            yield

        def head_norm(g, ob, gate_v, wn, out_v, tmps):
            for _ in head_norm_gen(g, ob, gate_v, wn, out_v, tmps):
                pass


        abes = contextlib.ExitStack()
        abes.__enter__()
        P.es = abes
        BMf = P.sb("BMf", [128, NS, 64], BF16)
        BMp = P.sb("BMp", [64, NS, 128], BF16)
        omix_dn = P.sb("omix_dn", [128, 4, LP + NS * LS], BF16)
        P.es = ges
        P.memset("pool", BMf[:], 1.0)
        P.memset("pool", BMp[:], 1.0)
        P.aselect(BMf[:], BMf[:], [[-4, NS], [1, 64]], ALU.is_ge, 0.0, 0, 0)
        P.aselect(BMf[:], BMf[:], [[4, NS], [-1, 64]], ALU.is_ge, 0.0, 3, 0)
        P.aselect(BMp[:], BMp[:], [[-4, NS], [0, 128]], ALU.is_ge, 0.0, 0, 1)
        P.aselect(BMp[:], BMp[:], [[4, NS], [0, 128]], ALU.is_ge, 0.0, 3, -1)

        with contextlib.ExitStack() as pes:
            P.es = pes
            rr["mm"][1] = [0, 1]
            NA = 2056
            wA = P.sb("w_inA", [128, 8, NA], BF16)
            for k2 in range(2):
                P.dma("pool", wA[:, 4 * k2:4 * k2 + 4, :],
                      View(w_in, w_in.h.rearrange("(k p) c -> p k c", p=128)[:, 4 * k2:4 * k2 + 4, 0:NA]))
            ntm = norm_tmps("A")
            xt = P.sb("A_xt", [128, 2, D], F32)
            hT = [P.sb("A_hT%d" % i, [128, TS_P], BF16) for i in range(8)]
            hT32 = [P.sb("A_hT32_%d" % i, [128, TS_P], F32) for i in range(8)]
            wab32 = P.sb("A_wab32", [128, 8, 8], F32)
            for k in range(8):
                P.dma("sp", wab32[:, k, :], w_in[k * 128:(k + 1) * 128, C_A0:C_A0 + 8], **NC_SLOW)
            hsave = P.sb("A_hsave", [128, 12, 3], F32)
            qkvs = P.sb("A_qkvs", [128, 12, TS_P], F32)
            sqb = ntm[0][:, 0:8 * TS_P].re("p (c t) -> p c t", t=TS_P)
            cv_in = qkvs[0:48, :, :].re("p a b -> p (a b)")
            cv_out = cv_in
            cvst = ntm[4][:, 0:576].re("p (c j) -> p c j", j=48)
            qkb2 = [P.sb("A_qkb%d" % i, [128, 8, TS_P], BF16) for i in range(2)]
            zs2 = [P.sb("A_zs%d" % i, [128, 4, TS_P], BF16) for i in range(2)]
            ab = P.sb("A_ab", [128, 8], F32)
            beta2 = [P.sb("A_beta%d" % i, [128, 4], F32) for i in range(2)]
            nbeta2 = [P.sb("A_nbeta%d" % i, [128, 4], F32) for i in range(2)]
            gg_ = P.sb("A_g", [128, 4], F32)
            tmp4 = P.sb("A_tmp4", [128, 4], F32)
            gU = P.sb("A_gU", [128, 4, 128], F32)
            eGrow2 = [P.sb("A_eGrow%d" % i, [128, 4, 128], F32) for i in range(2)]
            nGcol2 = [P.sb("A_nGcol%d" % i, [128, 4], F32) for i in range(2)]
            eGL = P.sb("A_eGL", [128, 4], F32)
            hn = (P.sb("A_hsq", [128, 512], BF16), P.sb("A_hsd", [128, 512], F32), P.sb("A_hrs", [128, 512], F32),
                  P.sb("A_hon", [128, 512], F32))
            n_sd = hn[1]
            n_rs = hn[2]
            NR = 4

            def rot(name, shape, dt):
                return [P.sb("A_%s%d" % (name, i), shape, dt) for i in range(NR)]
            xcl = [P.sb("A_xcl%d" % i, [128, 128], F32) for i in range(2)]
            ETs2 = [[P.sb("A_ETs%d_%d" % (p_, h_), [128, 128], BF16) for h_ in range(4)] for p_ in range(2)]
            ETi2 = [[P.sb("A_ETi%d_%d" % (p_, h_), [128, 128], BF16) for h_ in range(4)] for p_ in range(2)]
            Mp = rot("Mp", [128, 128], BF16); Np = rot("Np", [128, 128], BF16)
            QA = rot("QA", [128, 128], BF16); QB = rot("QB", [128, 128], BF16)
            QtY = rot("QtY", [128, 256], BF16); QtYb = rot("QtYb", [128, 256], BF16)
            attnT = rot("attnT", [128, 128], BF16)
            N0 = rot("N0", [128, 128], BF16); Xc = rot("Xc", [128, 128], BF16); Xn = rot("Xn", [128, 128], BF16)
            E1 = rot("E1", [128, 128], BF16); Et1 = rot("Et1", [128, 128], BF16); Zs = rot("Zs", [128, 256], BF16)
            Et2 = rot("Et2", [128, 128], BF16)
            Gsb2 = [P.sb("A_Gsb%d" % i, [128, 512], F32) for i in range(2)]
            bd32 = P.sb("A_bd32", [128, 128], BF16); m64 = P.sb("A_m64", [128, 128], BF16); m128 = P.sb("A_m128", [128, 128], BF16)
            bd64 = P.sb("A_bd64", [128, 128], BF16)
            for (mt, bs_) in ((bd32, 32), (bd64, 64)):
                P.memset("pool", mt[:], 1.0)
                v_ = mt[:].re("p (s t) -> p s t", t=bs_)
                P.aselect(v_, v_, [[-bs_, 128 // bs_], [0, bs_]], ALU.is_ge, 0.0, 0, 1)
                P.aselect(v_, v_, [[bs_, 128 // bs_], [0, bs_]], ALU.is_ge, 0.0, bs_ - 1, -1)
            P.tt("pool", m64[:], bd64[:], bd32[:], ALU.subtract)
            P.ts("pool", m128[:], bd64[:], -1.0, ALU.mult, 1.0, ALU.add)
            qdT2 = [rot("qdT%d_" % i, [128, 128], BF16) for i in range(2)]; kgT2 = [rot("kgT%d_" % i, [128, 128], BF16) for i in range(2)]
            kdec2 = [rot("kdec%d_" % i, [128, 128], BF16) for i in range(2)]; vsb2 = [rot("vsb%d_" % i, [128, 128], F32) for i in range(2)]
            Rp = rot("Rp", [128, 128], BF16); u_ = rot("u", [128, 128], BF16)
            Sf_p = [P.sb("A_Sf_p%d" % i, [128, 128], F32) for i in range(4)]
            Sb_p = [P.sb("A_Sb_p%d" % i, [128, 128], BF16) for i in range(4)]
            for i in range(4):
                P.memset("pool", Sf_p[i][:], 0.0)
                P.memset("pool", Sb_p[i][:], 0.0)
            Sf_s = P.sb("A_Sf_s", [128, NS, 128], F32)
            Sb_s = P.sb("A_Sb_s", [128, NS, 128], BF16)
            P.memset("pool", Sf_s[:], 0.0)
            kgT_m = P.sb("A_kgTm", [128, NS, 64], BF16)
            qdT_m = P.sb("A_qdTm", [128, NS, 64], BF16)
            kdec_m = P.sb("A_kdecm", [64, NS, 128], BF16)

            cb_all = P.sb("A_cb", [128, 12, 3 + TS_P], F32)
            stages = [(g_, st_) for g_ in (GP, GS) for st_ in range(g_.nst)]

            def front(g, st, par):
                C, TS, nseq, T = g.C, g.TS, g.nseq, g.T
                c = 0
                tsl = slice(0, C)
                qkb, zs, eGrow, beta, nbeta, Gsb, nGcol = qkb2[par], zs2[par], eGrow2[par], beta2[par], nbeta2[par], Gsb2[par], nGcol2[par]
                qdT, kgT, kdec, vsb = qdT2[par], kgT2[par], kdec2[par], vsb2[par]
                cb = cb_all[:, :, 0:nseq * (3 + T)].re("p c (s j) -> p c s j", j=3 + T)
                if st == 0:
                    cb = cb_all[:, :, 0:nseq * (3 + T)].re("p c (s j) -> p c s j", j=3 + T)
                    if nseq == 1:
                        P.memset("pool", cb[:, :, :, 0:3], 0.0)
                    else:
                        P.dma("sp", cv_in[:], convs[:])
                        for cc in range(12):
                            pb = bank("tr")
                            P.tr(pb[:, 0:48], cv_in[0:48, cc * 128:(cc + 1) * 128], ident_f[0:48, 0:48])
                            P.copy("act", cb[:, cc, :, 0:3], pb[:, 0:48].re("p (s j) -> p s j", j=3))
                    yield
                load_x(g, st, xt)
                norm_to_T(g, xt[0:C, 0, :], hT, 0, sc1T, 0, ntm, hT32=hT32)
                yield
                P.mark("A %s st%d normT done" % (g.name, st))
                if nseq == 1 and st > 0:
                    P.copy("pool", cb[:, :, 0, 0:3], hsave[:])
                for cc in range(16):
                    pb = bank("mm")
                    for k in range(8):
                        P.mm(pb[:, 0:TS], wA[:, k, cc * 128:(cc + 1) * 128], hT[k][:, 0:TS], start=(k == 0), stop=(k == 7))
                    if cc < 12:
                        P.copy("act", cb[:, cc, :, 3:3 + T], pb[:, 0:TS].re("p (s t) -> p s t", t=T))
                    else:
                        P.act(zs[:, cc - 12, 0:TS], pb[:, 0:TS], AF.Silu)
                    if cc % 2 == 1:
                        yield
                if nseq == 1:
                    P.copy("pool", hsave[:], cb[:, :, 0, T:T + 3])
                P.mark("A %s st%d inproj done" % (g.name, st))
                if st == g.nst - 1:
                    nr = nseq * 3
                    for cc in range(12):
                        P.copy("pool", cvst[:, cc, 0:nr].re("p (s j) -> p s j", j=3), cb[:, cc, :, T:T + 3])
                        pb = bank("tr")
                        P.tr(pb[0:nr, 0:128], cvst[:, cc, 0:nr], ident_f[:, :])
                        P.copy("act", cv_out[0:nr, cc * 128:(cc + 1) * 128], pb[0:nr, 0:128])
                    if nseq == 1:
                        P.dma("sp", conv_p[:], cv_out[0:3, :], is_output=True)
                    else:
                        P.dma("sp", conv_s[:], cv_out[0:48, :], is_output=True)
                yield
                for cc in range(12):
                    av = qkvs[:, cc, 0:TS].re("p (s t) -> p s t", t=T)
                    P.ts("dve", av, cb[:, cc, :, 0:T], wconvT[:, 0, cc:cc + 1], ALU.mult)
                    for i in range(1, 4):
                        P.stt(av, cb[:, cc, :, i:i + T], wconvT[:, i, cc:cc + 1], av, ALU.mult, ALU.add)
                    if cc % 2 == 1:
                        yield
                P.act(qkvs[:, :, 0:TS], qkvs[:, :, 0:TS], AF.Silu)
                P.mark("A %s st%d conv done" % (g.name, st))
                P.act(sqb[:, :, 0:TS], qkvs[:, 0:8, 0:TS], AF.Square)
                for half in range(2):
                    pb = bank("mm")
                    for j in range(4):
                        P.mm(pb[:, j * TS:(j + 1) * TS], ones_b[:, :], sqb[:, half * 4 + j, 0:TS])
                    W4 = 4 * TS
                    P.act(n_sd[:, 0:W4], pb[:, 0:W4], AF.Ln, bias=EPS)
                    P.act(n_rs[:, 0:W4], n_sd[:, 0:W4], AF.Exp, scale=-0.5)
                    qv = qkvs[:, half * 4:(half + 1) * 4, 0:TS]
                    P.stt(qv, qv, DKS if half == 0 else 1.0, n_rs[:, 0:W4].re("p (c t) -> p c t", t=TS), ALU.mult, ALU.mult)
                    yield
                P.copy("act", qkb[:, :, 0:TS], qkvs[:, 0:8, 0:TS])
                if nseq == 1 and st == 0:
                    P.dump("qkvs", qkvs[:], [128, 12, TS_P])
                P.mark("A %s st%d l2norm done" % (g.name, st))
                pb = bank("mm")
                for k in range(8):
                    P.mm(pb[0:C, 0:8], hT32[k][:, tsl], wab32[:, k, :], start=(k == 0), stop=(k == 7))
                P.copy("act", ab[0:C, :], pb[0:C, 0:8])
                P.act(beta[0:C, :], ab[0:C, 4:8], AF.Sigmoid)
                P.ts("dve", nbeta[0:C, :], beta[0:C, :], -1.0, ALU.mult)
                P.tt("dve", tmp4[0:C, :], ab[0:C, 0:4], dtb_bc[0:C, :], ALU.add)
                P.act(tmp4[0:C, :], tmp4[0:C, :], AF.Exp)
                P.act(tmp4[0:C, :], tmp4[0:C, :], AF.Ln, bias=1.0)
                P.tt("dve", gg_[0:C, :], tmp4[0:C, :], nexpa[0:C, :], ALU.mult)
                P.tt("dve", gU[0:C, :, 0:C], g.UI[:].un(1).bc([C, 4, C]), gg_[0:C, :].un(2).bc([C, 4, C]), ALU.mult)
                yield
                Gb = B[7]
                for h in range(4):
                    P.mm(Gb[:, h * C:(h + 1) * C], ones_f[0:C, :], gU[0:C, h, 0:C])
                pb = bank("tr")
                P.mm(pb[0:C, 0:4], g.UI[:], gg_[0:C, :])
                P.mm(pb[0:C, 8:12], g.LSm[:], gg_[0:C, :])
                P.ts("dve", nGcol[0:C, :], pb[0:C, 0:4], -1.0, ALU.mult)
                P.act(eGL[0:C, :], pb[0:C, 8:12], AF.Exp)
                P.act(eGrow[:, :, 0:C], Gb[:, 0:4 * C].re("p (h c) -> p h c", c=C), AF.Exp)
                P.copy("act", Gsb[0:C, 0:4 * C], Gb[0:C, 0:4 * C])
                for h in range(4):
                    xc = xcl[h % 2]
                    P.stt(xc[0:C, 0:C], Gsb[0:C, h * C:(h + 1) * C], nGcol[0:C, h:h + 1], g.UI[:], ALU.add, ALU.mult)
                    P.act(xc[0:C, 0:C], xc[0:C, 0:C], AF.Exp)
                    P.tt("pool", ETs2[par][h][0:C, 0:C], xc[0:C, 0:C], g.US[:], ALU.mult)
                    P.tt("pool", ETi2[par][h][0:C, 0:C], xc[0:C, 0:C], g.UI[:], ALU.mult)
                    yield
                yield
                for h in range(4):
                    r = h
                    P.tt("pool", qdT[r][:, 0:C], qkvs[:, h, tsl], eGrow[:, h, 0:C], ALU.mult)
                    P.tt("pool", kgT[r][:, 0:C], qkvs[:, 4 + h, tsl], eGrow[:, h, 0:C], ALU.mult)
                    pv = bank("tr")
                    P.tr(pv[0:C, 0:128], qkvs[:, 4 + h, tsl], ident_f[:, :])
                    P.tr(pv[0:C, 128:256], qkvs[:, 8 + h, tsl], ident_f[:, :])
                    P.act(kdec[r][0:C, :], pv[0:C, 0:128], AF.Identity, scale=eGL[0:C, h:h + 1])
                    P.copy("act", vsb[r][0:C, :], pv[0:C, 128:256])
                    yield
                yield

            def heads(g, st, par, nxt):
                C, TS, nseq, T = g.C, g.TS, g.nseq, g.T
                tsl = slice(0, C)
                gtok0 = g.tok0 + st * TS
                qkb, zs, eGrow, beta, nbeta, Gsb, nGcol = qkb2[par], zs2[par], eGrow2[par], beta2[par], nbeta2[par], Gsb2[par], nGcol2[par]
                qdT, kgT, kdec, vsb = qdT2[par], kgT2[par], kdec2[par], vsb2[par]
                ob = B[6]
                def dn_head(h):
                    r = h
                    hb = B[2 + h]
                    kTb = qkb[:, 4 + h, tsl]
                    qTb = qkb[:, h, tsl]
                    P.mm(hb[0:C, 0:C], kTb, kTb)
                    P.mm(hb[0:C, 128:128 + C], kTb, qTb)
                    yield
                    P.stt(Mp[r][0:C, 0:C], hb[0:C, 0:C], nbeta[0:C, h:h + 1], ETs2[par][h][0:C, 0:C], ALU.mult, ALU.mult)
                    P.tt("dve", attnT[r][0:C, 0:C], hb[0:C, 128:128 + C], ETi2[par][h][0:C, 0:C], ALU.mult)
                    yield
                    hbb = hb.bitcast(BF16)
                    P.tr(hbb[0:C, 0:C], Mp[r][0:C, 0:C], ident_b[0:C, 0:C])
                    yield
                    P.copy("act", Np[r][0:C, 0:C], hbb[0:C, 0:C])
                    yield

                    def neumann(Qv, QtYc, QtYn, nlev):
                        Q = Qv
                        cands = [QA[r], QB[r]]
                        for lv in range(nlev):
                            last = (lv == nlev - 1)
                            if last:
                                P.mm(hb[0:C, 128:128 + C], Q[0:C, 0:C], QtYc[0:C, 128:128 + C])
                            else:
                                P.mm(hb[0:C, 0:C], Q[0:C, 0:C], QtYc[0:C, 0:C])
                                P.mm(hb[0:C, 128:128 + C], Q[0:C, 0:C], QtYc[0:C, 128:128 + C])
                                P.mm(hb[0:C, 256:256 + C], QtYc[0:C, 0:C], Q[0:C, 0:C])
                            yield
                            P.tt("dve", QtYn[0:C, 128:128 + C], hb[0:C, 128:128 + C], QtYc[0:C, 128:128 + C], ALU.add)
                            if not last:
                                P.copy("act", QtYn[0:C, 0:C], hb[0:C, 0:C])
                                Qn = cands[lv % 2]
                                P.copy("dve", Qn[0:C, 0:C], hb[0:C, 256:256 + C])
                                Q = Qn
                            yield
                            QtYc, QtYn = QtYn, QtYc
                        res_[0] = (QtYc, QtYn)
                    res_ = [None]
                    QtYc = QtY[r]; QtYn = QtYb[r]
                    P.copy("pool", QtYc[0:C, 128:128 + C], ident_b[0:C, 0:C])
                    if C == 128:
                        P.tt("pool", QtYc[:, 0:128], Mp[r][:, :], bd32[:], ALU.mult)
                        P.tt("pool", N0[r][:, :], Np[r][:, :], bd32[:], ALU.mult)
                        P.tt("pool", E1[r][:, :], Mp[r][:, :], m64[:], ALU.mult)
                        P.tt("pool", Et1[r][:, :], Np[r][:, :], m64[:], ALU.mult)
                        P.tt("pool", Et2[r][:, :], Np[r][:, :], m128[:], ALU.mult)
                        yield
                        yield from neumann(N0[r], QtYc, QtYn, 5)
                        QtYc, QtYn = res_[0]
                        Ycur = QtYc[:, 128:256]
                        P.tr(hbb[:, 0:128], Ycur, ident_b[:, :])
                        yield
                        P.copy("act", Xc[r][:, :], hbb[:, 0:128])
                        yield
                        P.mm(hb[:, 128:256], E1[r][:, :], Xc[r][:, :])
                        P.mm(hb[:, 256:384], Et1[r][:, :], Ycur)
                        yield
                        P.copy("act", Zs[r][:, 0:256], hb[:, 128:384])
                        yield
                        P.mm(hb[:, 0:128], Ycur, Zs[r][:, 0:128])
                        P.mm(hb[:, 128:256], Xc[r][:, :], Zs[r][:, 128:256])
                        yield
                        P.tt("dve", Xn[r][:, :], hb[:, 0:128], Xc[r][:, :], ALU.add)
                        P.tt("dve", QtYn[:, 128:256], hb[:, 128:256], Ycur, ALU.add)
                        yield
                        Ycur = QtYn[:, 128:256]
                        QtYc, QtYn = QtYn, QtYc
                        P.mm(hb[:, 256:384], Et2[r][:, :], Ycur)
                        yield
                        P.copy("act", Zs[r][:, 128:256], hb[:, 256:384])
                        yield
                        P.mm(hb[:, 128:256], Xn[r][:, :], Zs[r][:, 128:256])
                        yield
                        P.tt("dve", QtYn[:, 128:256], hb[:, 128:256], Ycur, ALU.add)
                        Y = QtYn[:, 128:256]
                    else:
                        P.copy("pool", QtYc[0:C, 0:C], Mp[r][0:C, 0:C])
                        yield
                        yield from neumann(Np[r], QtYc, QtYn, 2)
                        QtYc, QtYn = res_[0]
                        Y = QtYc[0:C, 128:128 + C]
                    yield
                    if nseq == 1:
                        Sf = [Sf_p[h][:, :]]; Sb = [Sb_p[h][:, :]]
                        kgs = [kgT[r][:, 0:C]]; qds = [qdT[r][:, 0:C]]; kds = [kdec[r][0:C, :]]
                    else:
                        P.dma("sp", Sf_s[:], View(sdn, sdn.h[:, h, :, :].rearrange("s k v -> k s v")))
                        P.copy("act", Sb_s[:], Sf_s[:])
                        P.tt("dve", kgT_m[:], kgT[r][:, 0:C].un(1).bc([128, NS, C]), BMf[:], ALU.mult)
                        P.tt("dve", qdT_m[:], qdT[r][:, 0:C].un(1).bc([128, NS, C]), BMf[:], ALU.mult)
                        P.tt("dve", kdec_m[:], kdec[r][0:C, :].un(1).bc([C, NS, 128]), BMp[:], ALU.mult)
                        Sf = [Sf_s[:, s, :] for s in range(NS)]; Sb = [Sb_s[:, s, :] for s in range(NS)]
                        kgs = [kgT_m[:, s, :] for s in range(NS)]; qds = [qdT_m[:, s, :] for s in range(NS)]
                        kds = [kdec_m[:, s, :] for s in range(NS)]
                    for s in range(nseq):
                        P.mm(hb[0:C, 0:128], kgs[s], Sb[s], start=(s == 0), stop=(s == nseq - 1))
                    yield
                    P.tt("dve", Rp[r][0:C, :], vsb[r][0:C, :], hb[0:C, 0:128], ALU.subtract)
                    yield
                    P.mm(hb[0:C, 128:256], Y, Rp[r][0:C, :])
                    yield
                    P.act(u_[r][0:C, :], hb[0:C, 128:256], AF.Identity, scale=beta[0:C, h:h + 1])
                    yield
                    for s in range(nseq):
                        P.mm(ob[:, h * C:(h + 1) * C], Sb[s], qds[s], start=(s == 0), stop=False)
                    P.mm(ob[:, h * C:(h + 1) * C], u_[r][0:C, :], attnT[r][0:C, 0:C], start=False, stop=True)
                    for s0 in range(0, nseq, 4):
                        for s in range(s0, min(s0 + 4, nseq)):
                            P.mm(hb[:, (s - s0) * 128:(s - s0 + 1) * 128], kds[s], u_[r][0:C, :])
                        yield
                        for s in range(s0, min(s0 + 4, nseq)):
                            lastc = (s + 1) * T - 1 if nseq > 1 else C - 1
                            P.stt(Sf[s], Sf[s], eGrow[:, h, lastc:lastc + 1], hb[:, (s - s0) * 128:(s - s0 + 1) * 128],
                                  ALU.mult, ALU.add)
                            if nseq == 1:
                                P.copy("pool", Sb[s], Sf[s])
                        yield
                    if nseq > 1:
                        P.dma("sp", View(dn_s, dn_s.h[:, h, :, :].rearrange("s k v -> k s v")), Sf_s[:], is_output=True)

                gens = [dn_head(h) for h in range(4)]
                if nseq > 1:
                    for gen in gens:
                        for _ in gen:
                            pass
                    if nxt is not None:
                        for _ in nxt:
                            pass
                else:
                    if nxt is not None:
                        gens.append(nxt)
                    while gens:
                        for gen in list(gens):
                            try:
                                next(gen)
                            except StopIteration:
                                gens.remove(gen)
                head_norm(g, ob, zs[:, :, tsl], wdnn, omix_dn[:, :, gtok0:gtok0 + C], hn)
                if nseq == 1 and st == g.nst - 1:
                    for i in range(4):
                        P.dma("sp", View(dn_p, dn_p.h[i, :, :]), Sf_p[i][:], is_output=True)

            f0 = front(stages[0][0], stages[0][1], 0)
            for _ in f0:
                pass
            for si, (g_, st_) in enumerate(stages):
                nxt = front(stages[si + 1][0], stages[si + 1][1], (si + 1) % 2) if si + 1 < len(stages) else None
                heads(g_, st_, si % 2, nxt)
            P.emit_phase()
        P.es = ges

        with contextlib.ExitStack() as pes:
            P.es = pes
            rr["mm"][1] = [0, 1]
            NB = NCOL - C_GQ
            wB = P.sb("w_inB", [128, 8, NB], BF16)
            for k2 in range(2):
                P.dma("pool", wB[:, 4 * k2:4 * k2 + 4, :],
                      View(w_in, w_in.h.rearrange("(k p) c -> p k c", p=128)[:, 4 * k2:4 * k2 + 4, C_GQ:NCOL]))
            wo = P.sb("w_o_sb", [128, 8, D], BF16)
            P.dma("pool", wo[:], View(w_o, w_o.h.rearrange("(k p) c -> p k c", p=128)))
            oGQ, oGK, oGV, oGR, oGG = 0, 512, 1024, 1536, 2048
            ntm = norm_tmps("B")
            xt2 = [P.sb("B_xt%d" % i, [128, 1, D], F32) for i in range(2)]
            hT = [P.sb("B_hT%d" % i, [128, TS_P], BF16) for i in range(8)]
            gq2 = [P.sb("B_gq%d" % i, [128, 4, TS_P], F32) for i in range(2)]
            gk2 = [P.sb("B_gk%d" % i, [128, 4, TS_P], F32) for i in range(2)]
            grs2 = [P.sb("B_grs%d" % i, [128, 4, TS_P], BF16) for i in range(2)]
            ggT = P.sb("B_ggT", [16, TS_P], BF16)
            gk_tok = P.sb("B_gktok", [128, 512], F32)
            gv_tok2 = [P.sb("B_gvtok%d" % i, [128, 512], BF16) for i in range(2)]
            glog2 = [P.sb("B_glog%d" % i, [128, 512], F32) for i in range(2)]
            eBL = P.sb("B_eBL", [128, 512], F32)
            kdec_g2 = [P.sb("B_kdecg%d" % i, [128, 512], BF16) for i in range(2)]
            eP = [P.sb("B_eP%d" % i, [128, 128], F32) for i in range(4)]
            eN = [P.sb("B_eN%d" % i, [128, 128], F32) for i in range(4)]
            gqd = [P.sb("B_gqd%d" % i, [128, 128], BF16) for i in range(4)]
            gki = [P.sb("B_gki%d" % i, [128, 128], BF16) for i in range(4)]
            attg = [P.sb("B_attg%d" % i, [128, 128], BF16) for i in range(4)]
            omix_g2 = [P.sb("B_omixg%d" % i, [128, 4, TS_P], BF16) for i in range(2)]
            hn = (P.sb("B_hsq", [128, 512], BF16), P.sb("B_hsd", [128, 512], F32), P.sb("B_hrs", [128, 512], F32),
                  P.sb("B_hon", [128, 512], F32))
            Sf_p = [P.sb("B_Sf_p%d" % i, [128, 128], F32) for i in range(4)]
            Sb_p = [P.sb("B_Sb_p%d" % i, [128, 128], BF16) for i in range(4)]
            for i in range(4):
                P.memset("pool", Sf_p[i][:], 0.0)
                P.memset("pool", Sb_p[i][:], 0.0)
            Sf_s = P.sb("B_Sf_s", [128, NS, 128], F32)
            Sb_s = P.sb("B_Sb_s", [128, NS, 128], BF16)
            P.memset("pool", Sf_s[:], 0.0)
            gqd_m = P.sb("B_gqdm", [128, NS, 64], BF16)
            kdg_m = P.sb("B_kdgm", [64, NS, 128], BF16)
            mixt2 = [P.sb("B_mixt%d" % i, [128, 512], F32) for i in range(2)]
            stagesB = [(g_, st_) for g_ in (GP, GS) for st_ in range(g_.nst)]

            def frontB_(g, st, par):
                C, TS, nseq, T = g.C, g.TS, g.nseq, g.T
                tsl = slice(0, C)
                xt, gq, gk, grs, glog, kdec_g, gv_tok = xt2[par], gq2[par], gk2[par], grs2[par], glog2[par], kdec_g2[par], gv_tok2[par]
                load_x(g, st, xt)
                norm_to_T(g, xt[0:C, 0, :], hT, 0, sc1T, 0, ntm)
                yield
                for cc in range(16):
                    if 8 <= cc < 12:
                        continue
                    pb = bank("mm")
                    for k in range(8):
                        P.mm(pb[:, 0:TS], wB[:, k, cc * 128:(cc + 1) * 128], hT[k][:, 0:TS], start=(k == 0), stop=(k == 7))
                    if cc < 4:
                        P.copy("act", gq[:, cc, 0:TS], pb[:, 0:TS])
                    elif cc < 8:
                        P.copy("act", gk[:, cc - 4, 0:TS], pb[:, 0:TS])
                    else:
                        P.act(grs[:, cc - 12, 0:TS], pb[:, 0:TS], AF.Silu)
                    if cc % 2 == 1:
                        yield
                pb = bank("mm")
                for k in range(8):
                    P.mm(pb[0:16, 0:TS], wB[:, k, oGG:oGG + 16], hT[k][:, 0:TS], start=(k == 0), stop=(k == 7))
                P.copy("act", ggT[:, 0:TS], pb[0:16, 0:TS])
                yield
                for (off, dst) in ((oGK, gk_tok), (oGV, gv_tok)):
                    pb = bank("mm")
                    for k in range(8):
                        P.mm(pb[0:C, :], hT[k][:, tsl], wB[:, k, off:off + 512], start=(k == 0), stop=(k == 7))
                    P.copy("act", dst[0:C, :], pb[0:C, :])
                    yield
                pb = bank("mm")
                P.mm(pb[0:C, :], ggT[0:16, tsl], wg2_b[:, :], start=True, stop=False)
                P.mm(pb[0:C, :], ones_b[0:1, 0:C], bg_b[0:1, :], start=False, stop=True)
                P.act(glog[0:C, :], pb[0:C, :], AF.Exp, scale=-1.0)
                P.act(glog[0:C, :], glog[0:C, :], AF.Ln, bias=1.0)
                P.ts("dve", glog[0:C, :], glog[0:C, :], -1.0 / 16.0, ALU.mult)
                yield
                pb = bank("mm")
                P.mm(pb[0:C, :], g.LSm[:], glog[0:C, :])
                P.act(eBL[0:C, :], pb[0:C, :], AF.Exp)
                P.tt("dve", kdec_g[0:C, :], gk_tok[0:C, :], eBL[0:C, :], ALU.mult)
                yield

            def headsB_(g, st, par, nxt, post_prev=None):
                C, TS, nseq, T = g.C, g.TS, g.nseq, g.T
                c = 0
                nch = 1
                tsl = slice(0, C)
                gtok0 = g.tok0 + st * TS
                xt, gq, gk, grs, glog, kdec_g, gv_tok = xt2[par], gq2[par], gk2[par], grs2[par], glog2[par], kdec_g2[par], gv_tok2[par]
                ob = B[6 + par]
                def gla_head(h):
                    hb = B[2 + h]
                    hs = slice(h * 128, (h + 1) * 128)
                    P.mm(hb[:, 0:C], glog[0:C, hs], g.UI[:])
                    yield
                    P.act(eP[h][:, 0:C], hb[:, 0:C], AF.Exp)
                    P.act(eN[h][:, 0:C], hb[:, 0:C], AF.Exp, scale=-1.0)
                    yield
                    P.stt(gqd[h][:, 0:C], gq[:, h, tsl], DKS, eP[h][:, 0:C], ALU.mult, ALU.mult)
                    P.tt("pool", gki[h][:, 0:C], gk[:, h, tsl], eN[h][:, 0:C], ALU.mult)
                    yield
                    P.mm(hb[0:C, 128:128 + C], gki[h][:, 0:C], gqd[h][:, 0:C])
                    yield
                    P.tt("dve", attg[h][0:C, 0:C], hb[0:C, 128:128 + C], g.UI[:], ALU.mult)
                    yield
                    if nseq == 1:
                        Sf = [Sf_p[h][:, :]]; Sb = [Sb_p[h][:, :]]
                        qds = [gqd[h][:, 0:C]]; kds = [kdec_g[0:C, hs]]
                    else:
                        P.dma("sp", Sf_s[:], View(sgla, sgla.h[:, h, :, :].rearrange("s k v -> k s v")))
                        P.copy("act", Sb_s[:], Sf_s[:])
                        P.tt("dve", gqd_m[:], gqd[h][:, 0:C].un(1).bc([128, NS, C]), BMf[:], ALU.mult)
                        P.tt("dve", kdg_m[:], kdec_g[0:C, hs].un(1).bc([C, NS, 128]), BMp[:], ALU.mult)
                        Sf = [Sf_s[:, s, :] for s in range(NS)]; Sb = [Sb_s[:, s, :] for s in range(NS)]
                        qds = [gqd_m[:, s, :] for s in range(NS)]; kds = [kdg_m[:, s, :] for s in range(NS)]
                    for s in range(nseq):
                        P.mm(ob[:, h * C:(h + 1) * C], Sb[s], qds[s], start=(s == 0), stop=False)
                    P.mm(ob[:, h * C:(h + 1) * C], gv_tok[0:C, hs], attg[h][0:C, 0:C], start=False, stop=True)
                    for s0 in range(0, nseq, 4):
                        for s in range(s0, min(s0 + 4, nseq)):
                            P.mm(hb[:, (s - s0) * 128:(s - s0 + 1) * 128], kds[s], gv_tok[0:C, hs])
                        yield
                        for s in range(s0, min(s0 + 4, nseq)):
                            lastc = (s + 1) * T - 1 if nseq > 1 else C - 1
                            P.stt(Sf[s], Sf[s], eP[h][:, lastc:lastc + 1], hb[:, (s - s0) * 128:(s - s0 + 1) * 128],
                                  ALU.mult, ALU.add)
                            if nseq == 1:
                                P.copy("pool", Sb[s], Sf[s])
                        yield
                    if nseq > 1:
                        P.dma("sp", View(gla_s, gla_s.h[:, h, :, :].rearrange("s k v -> k s v")), Sf_s[:], is_output=True)

                gens = [gla_head(h) for h in range(4)]
                extras = [x for x in (post_prev, nxt) if x is not None]
                if nseq > 1:
                    for gen in extras + gens:
                        for _ in gen:
                            pass
                else:
                    pending = nxt
                    if post_prev is not None:
                        gens = gens + [post_prev]
                    elif pending is not None:
                        gens = gens + [pending]
                        pending = None
                    while gens:
                        for gen in list(gens):
                            try:
                                next(gen)
                            except StopIteration:
                                gens.remove(gen)
                                if gen is post_prev and pending is not None:
                                    gens.append(pending)
                                    pending = None
                if nseq == 1 and st == g.nst - 1:
                    for i in range(4):
                        P.dma("sp", View(gla_p, gla_p.h[i, :, :]), Sf_p[i][:], is_output=True)

            def postB_(g, st, par):
                C, TS, nseq, T = g.C, g.TS, g.nseq, g.T
                c = 0
                nch = 1
                tsl = slice(0, C)
                gtok0 = g.tok0 + st * TS
                xt, grs = xt2[par], grs2[par]
                omix_g = omix_g2[par]
                ob = B[6 + par]
                yield from head_norm_gen(g, ob, grs[:, :, tsl], wglan, omix_g[:, :, tsl], hn, via_act=(par == 1))
                for n in range(2):
                    pb = bank("mm")
                    for k in range(8):
                        lhs = omix_dn[:, k, gtok0:gtok0 + C] if k < 4 else omix_g[:, k - 4, tsl]
                        P.mm(pb[0:C, :], lhs, wo[:, k, n * 512:(n + 1) * 512], start=(k == 0), stop=(k == 7))
                    yield
                    P.tt("dve", mixt2[n][0:C, :], pb[0:C, :], g.gtok[:, 0, n * 512:(n + 1) * 512], ALU.mult)
                    yield
                    P.tt("dve", xt[0:C, c, n * 512:(n + 1) * 512], xt[0:C, c, n * 512:(n + 1) * 512], mixt2[n][0:C, :], ALU.add)
                    yield
                r0 = st * TS
                P.dma("sp", View(g.x1d, g.x1d.h[r0:r0 + TS, :].rearrange("(c p) d -> p c d", p=C)), xt[0:C, 0:nch, :],
                      owner=xt)

            fb0 = frontB_(stagesB[0][0], stagesB[0][1], 0)
            for _ in fb0:
                pass
            post_prev = None
            for si, (g_, st_) in enumerate(stagesB):
                nxt = frontB_(stagesB[si + 1][0], stagesB[si + 1][1], (si + 1) % 2) if si + 1 < len(stagesB) else None
                headsB_(g_, st_, si % 2, nxt, post_prev)
                post_prev = postB_(g_, st_, si % 2)
            for _ in post_prev:
                pass
            P.emit_phase()
        P.es = ges

        abes.close()
        NTOK = LP + NS * LS
        NCHK = 17
        with contextlib.ExitStack() as pes:
            P.es = pes
            rr["mm"][1] = [2, 3, 6]
            h2T = [P.sb("h2T%d" % i, [128, NTOK], BF16) for i in range(8)]
            acc = [P.sb("x2acc%d" % i, [128, D], F32) for i in range(NCHK)]
            ntm = norm_tmps("M")
            xt = [P.sb("M_xt%d" % i, [128, 1, D], F32) for i in range(2)]
            chunks = [(GP, i) for i in range(16)] + [(GS, 0)]

            def prep(ci):
                g, i = chunks[ci]
                C = g.C
                x_ = xt[ci % 2]
                P.dma("sp", x_[0:C, 0, :], View(g.x1d, g.x1d.h[i * C:(i + 1) * C, :]))
                norm_to_T(g, x_[0:C, 0, :], h2T, g.tok0 + i * C, sc2T, 24, ntm)
            NE = 8
            FE = 4
            wu = [P.sb("wu%d" % i, [128, 8, 512], BF16) for i in range(2)]
            wd = [P.sb("wd%d" % i, [128, FE, D], BF16) for i in range(2)]
            actT = P.sb("actT", [128, FE, NTOK], BF16)
            sqt = [P.sb("M_sq%d" % i, [128, 512], BF16) for i in range(2)]
            ttiles = [(t0, min(512, NTOK - t0)) for t0 in range(0, NTOK, 512)]
            wnf = P.sb("wnf", [128, D], F32)
            P.dma("sp", wnf[:], bcast_rows(w_norm_f, 128, D))
            fss2 = [P.sb("F_ss%d" % i, [128, 1], F32) for i in range(2)]
            fsd2 = [P.sb("F_sd%d" % i, [128, 1], F32) for i in range(2)]
            frs2 = [P.sb("F_rs%d" % i, [128, 1], F32) for i in range(2)]
            fjunk2 = [ntm[0], P.sb("F_junk1", [128, D], BF16)]

            def final_load(ci):
                g, i = chunks[ci]
                P.dma("sp", xt[ci % 2][0:g.C, 0, :], View(g.x1d, g.x1d.h[i * g.C:(i + 1) * g.C, :]))

            def final(ci):
                g, i = chunks[ci]
                C = g.C
                x_ = xt[ci % 2]
                if ci == 0:
                    final_load(0)
                if ci + 1 < len(chunks):
                    final_load(ci + 1)
                a_ = acc[ci][0:C, :]
                fss, fsd, frs, fjunk = fss2[ci % 2], fsd2[ci % 2], frs2[ci % 2], fjunk2[ci % 2]
                P.tt("dve", a_, a_, g.gtok[:, 1, :], ALU.mult)
                P.tt("dve" if ci % 2 == 0 else "pool", a_, a_, x_[0:C, 0, :], ALU.add)
                P.act(fjunk[0:C, :], a_, AF.Square, accum=fss[0:C, :])
                P.act(fsd[0:C, :], fss[0:C, :], AF.Ln, bias=EPS, scale=1.0 / D)
                P.act(frs[0:C, :], fsd[0:C, :], AF.Exp, scale=-0.5)
                P.stt(x_[0:C, 0, :], a_, frs[0:C, :], wnf[0:C, :], ALU.mult, ALU.mult)
                P.dma("sp", View(g.y, g.y.h[i * C:(i + 1) * C, :]), x_[0:C, 0, :], is_output=True)

            ei = [0]

            def up(wub, f, t0, tw):
                pb = bank("mm")
                for k in range(8):
                    P.mm(pb[:, 0:tw], wub[:, k, f * 128:(f + 1) * 128], h2T[k][:, t0:t0 + tw], start=(k == 0), stop=(k == 7))
                sq = sqt[ei[0] % 2]
                ei[0] += 1
                P.act(sq[:, 0:tw], pb[:, 0:tw], AF.Square)
                P.stt(actT[:, f, t0:t0 + tw], pb[:, 0:tw], 0.0, sq[:, 0:tw], ALU.is_gt, ALU.mult)

            for e in range(NE):
                wub, wdb = wu[e % 2], wd[e % 2]
                P.dma("pool", wub[:], View(w_up, w_up.h.rearrange("(k p) c -> p k c", p=128)[:, :, e * 512:(e + 1) * 512]))
                P.dma("pool", wdb[:], View(w_down, w_down.h.rearrange("(f p) c -> p f c", p=128)[:, e * FE:(e + 1) * FE, :]))
                if e == 0:
                    for ti, (t0, tw) in enumerate(ttiles):
                        for ci in range(len(chunks)):
                            g, i = chunks[ci]
                            if t0 <= g.tok0 + i * g.C < t0 + tw:
                                prep(ci)
                        for f in range(FE):
                            up(wub, f, t0, tw)
                else:
                    for f in range(FE):
                        for (t0, tw) in ttiles:
                            up(wub, f, t0, tw)
                for ci, (g, i) in enumerate(chunks):
                    C = g.C
                    tk = g.tok0 + i * C
                    for n in range(2):
                        pb = bank("tr") if (ci * 2 + n) % 2 == 0 else bank("rc")
                        for f in range(FE):
                            P.mm(pb[0:C, :], actT[:, f, tk:tk + C], wdb[:, f, n * 512:(n + 1) * 512], start=(f == 0), stop=(f == FE - 1))
                        dst = acc[ci][0:C, n * 512:(n + 1) * 512]
                        if e == 0:
                            P.copy("act", dst, pb[0:C, :])
                        else:
                            P.tt("dve", dst, dst, pb[0:C, :], ALU.add)
                    if e == NE - 1:
                        final(ci)
            P.emit_phase(final=True)
        P.es = ges
        print("instructions:", P.ninstr)
    return nc


_NC = None


def kernel(x_prompt, x_sample, state_dn_conv, state_dn, state_gla, c_prompt, c_sample, w_ada, b_ada,
           w_norm1, w_in, w_conv, dn_a_log, dn_dt_bias, w_dn_norm, w_gla_g2, b_gla_g, w_gla_norm, w_o,
           w_norm2, w_up, w_down, w_norm_f):
    global _NC
    if _NC is None:
        _NC = build()
    nc = _NC
    f = lambda a: np.ascontiguousarray(np.asarray(a, dtype=np.float32))
    shared = {
        "w_ada": f(w_ada[0]), "b_ada": f(b_ada[0]), "w_norm1": f(w_norm1[0]), "w_in": f(w_in[0]),
        "w_conv": f(w_conv[0]), "dn_a_log": f(dn_a_log[0]), "dn_dt_bias": f(dn_dt_bias[0]),
        "w_dn_norm": f(w_dn_norm[0]), "w_gla_g2": f(w_gla_g2[0]), "b_gla_g": f(b_gla_g[0]),
        "w_gla_norm": f(w_gla_norm[0]), "w_o": f(w_o[0]), "w_norm2": f(w_norm2[0]), "w_up": f(w_up[0]),
        "w_down": f(w_down[0]), "w_norm_f": f(w_norm_f),
    }
    in_maps = []
    for c in range(8):
        sl = slice(NS * c, NS * (c + 1))
        m = dict(shared)
        m["xp"] = f(x_prompt[c])
        m["xs"] = f(np.asarray(x_sample)[sl].reshape(NS * LS, D))
        m["convs"] = f(np.asarray(state_dn_conv)[0, sl].reshape(NS * 3, 1536))
        m["sdn"] = f(np.asarray(state_dn)[0, sl])
        m["sgla"] = f(np.asarray(state_gla)[0, sl])
        m["cvec"] = f(np.concatenate([np.asarray(c_prompt)[c:c + 1], np.asarray(c_sample)[sl]], axis=0))
        in_maps.append(m)
    res = run_bass_kernel_spmd(nc, in_maps, core_ids=list(range(8)))
    R = res.results
    y_prompt = np.stack([R[c]["y_p"] for c in range(8)], axis=0)
    y_sample = np.concatenate([R[c]["y_s"].reshape(NS, LS, D) for c in range(8)], axis=0)
    conv_pp = np.stack([R[c]["conv_p"] for c in range(8)], axis=0)[None]
    dn_pp = np.stack([R[c]["dn_p"] for c in range(8)], axis=0)[None]
    gla_pp = np.stack([R[c]["gla_p"] for c in range(8)], axis=0)[None]
    conv_ss = np.concatenate([R[c]["conv_s"].reshape(NS, 3, 1536) for c in range(8)], axis=0)[None]
    dn_ss = np.concatenate([R[c]["dn_s"] for c in range(8)], axis=0)[None]
    gla_ss = np.concatenate([R[c]["gla_s"] for c in range(8)], axis=0)[None]
    outs = (y_prompt, y_sample, conv_pp, dn_pp, gla_pp, conv_ss, dn_ss, gla_ss)
    return tuple(np.ascontiguousarray(o, dtype=np.float32) for o in outs)
```
